# Optimizing a Trainium2 kernel written in Bass

```python
import jax, jax.numpy as jnp
from jax import lax
import numpy as np

D_MODEL = 2048
BATCH = 2
SEQ = 4096
DEPTH = 1

MIX_WIDTH = D_MODEL
A_WIDTH = MIX_WIDTH // 2
B_WIDTH = MIX_WIDTH - A_WIDTH
CHUNK = 128
A_HEAD_DIM = 128
A_HEADS = A_WIDTH // A_HEAD_DIM
POOL_WINDOWS = (2, 4, 8, 16)
B_GROUPS = len(POOL_WINDOWS)
B_GROUP_DIM = B_WIDTH // B_GROUPS
IN_WIDTH = 2 * A_WIDTH + B_WIDTH
D_FF = 4 * D_MODEL
EPS = 1e-6

kernel_name = "hybrid_sgu_pool_block"


def rmsnorm(x, g):
    xf = x.astype(jnp.float32)
    y = xf * lax.rsqrt(jnp.mean(xf * xf, axis=-1, keepdims=True) + EPS)
    return (y * g.astype(jnp.float32)).astype(x.dtype)


def spatial_gating(u, v, w_s, b_s, g_v):
    bsz, s, _ = u.shape
    n_chunks = s // CHUNK
    vh = rmsnorm(v.reshape(bsz, s, A_HEADS, A_HEAD_DIM), g_v.reshape(A_HEADS, A_HEAD_DIM))
    vh = vh.reshape(bsz, n_chunks, CHUNK, A_HEADS, A_HEAD_DIM)
    causal = jnp.tril(jnp.ones((CHUNK, CHUNK), dtype=bool))
    w = jnp.where(causal[None], w_s, jnp.zeros_like(w_s))
    mixed = jnp.einsum('hts,bcshd->bcthd', w, vh)
    mixed = mixed + jnp.transpose(b_s)[None, None, :, :, None]
    return u * mixed.reshape(bsz, s, A_WIDTH)


def multiscale_pool(z, w_pool, pool_scale):
    bsz, s, _ = z.shape
    zf = z.astype(jnp.float32).reshape(bsz, s, B_GROUPS, B_GROUP_DIM)
    csum = jnp.cumsum(zf, axis=1)
    cpad = jnp.concatenate([jnp.zeros_like(csum[:, :1]), csum], axis=1)
    pos = jnp.arange(s, dtype=jnp.int32)
    outs = []
    for g, win in enumerate(POOL_WINDOWS):
        c = cpad[:, :, g]
        lag = jnp.pad(c, ((0, 0), (win - 1, 0), (0, 0)))[:, :s]
        count = jnp.minimum(pos + 1, win).astype(jnp.float32)[None, :, None]
        outs.append((c[:, 1:] - lag) / count - zf[:, :, g])
    pooled = jnp.stack(outs, axis=2).astype(z.dtype)
    y = jnp.einsum('bsgc,gcd->bsgd', pooled, w_pool).reshape(bsz, s, B_WIDTH)
    return y * pool_scale


def setup_inputs(seed: int = 0) -> dict:
    key = jax.random.key(seed)
    ks = jax.random.split(key, 16)
    f32 = jnp.float32
    x = jax.random.normal(ks[0], (BATCH, SEQ, D_MODEL), f32)
    g_mix = 1.0 + 0.05 * jax.random.normal(ks[1], (DEPTH, D_MODEL), f32)
    w_in = jax.random.normal(ks[2], (DEPTH, D_MODEL, IN_WIDTH), f32) * D_MODEL ** -0.5
    g_v = 1.0 + 0.05 * jax.random.normal(ks[3], (DEPTH, A_WIDTH), f32)
    w_s = jax.random.normal(ks[4], (DEPTH, A_HEADS, CHUNK, CHUNK), f32) * (0.5 * CHUNK ** -0.5)
    b_s = 1.0 + 0.1 * jax.random.normal(ks[5], (DEPTH, A_HEADS, CHUNK), f32)
    w_pool = jax.random.normal(ks[6], (DEPTH, B_GROUPS, B_GROUP_DIM, B_GROUP_DIM), f32) * B_GROUP_DIM ** -0.5
    pool_scale = 0.5 + 0.1 * jax.random.normal(ks[7], (DEPTH, B_WIDTH), f32)
    w_out = jax.random.normal(ks[8], (DEPTH, MIX_WIDTH, D_MODEL), f32) * MIX_WIDTH ** -0.5
    g_ffn = 1.0 + 0.05 * jax.random.normal(ks[9], (DEPTH, D_MODEL), f32)
    w_up = jax.random.normal(ks[10], (DEPTH, D_MODEL, D_FF), f32) * D_MODEL ** -0.5
    w_down = jax.random.normal(ks[11], (DEPTH, D_FF, D_MODEL), f32) * D_FF ** -0.5
    g_final = 1.0 + 0.05 * jax.random.normal(ks[12], (D_MODEL,), f32)
    return {"x": x, "g_mix": g_mix, "w_in": w_in, "g_v": g_v, "w_s": w_s,
            "b_s": b_s, "w_pool": w_pool, "pool_scale": pool_scale,
            "w_out": w_out, "g_ffn": g_ffn, "w_up": w_up, "w_down": w_down,
            "g_final": g_final}


def reference(x, g_mix, w_in, g_v, w_s, b_s, w_pool, pool_scale, w_out,
              g_ffn, w_up, w_down, g_final):
    for layer in range(DEPTH):
        h = rmsnorm(x, g_mix[layer])
        proj = jnp.einsum('bsd,de->bse', h, w_in[layer])
        u = jax.nn.gelu(proj[..., :A_WIDTH])
        v = jax.nn.gelu(proj[..., A_WIDTH:2 * A_WIDTH])
        z = proj[..., 2 * A_WIDTH:]
        out_a = spatial_gating(u, v, w_s[layer], b_s[layer], g_v[layer])
        out_b = multiscale_pool(z, w_pool[layer], pool_scale[layer])
        mixed = jnp.concatenate([out_a, out_b], axis=-1)
        x = x + jnp.einsum('bse,ed->bsd', mixed, w_out[layer])
        h = rmsnorm(x, g_ffn[layer])
        act = jnp.square(jax.nn.relu(jnp.einsum('bsd,df->bsf', h, w_up[layer])))
        x = x + jnp.einsum('bsf,fd->bsd', act, w_down[layer])
    return rmsnorm(x, g_final)
```

```python
from contextlib import ExitStack

import numpy as np
import concourse.bass as bass
import concourse.mybir as mybir
from concourse.bass_utils import run_bass_kernel_spmd

F32 = mybir.dt.float32
BF16 = mybir.dt.bfloat16
AF = mybir.ActivationFunctionType
ALU = mybir.AluOpType
AX = mybir.AxisListType

NCORES = 8
D = 2048
TOK = 1024
SEQ = 4096
KC = 16
HALO = 16
ZL = TOK + HALO
DFF = 8192
EPS = 1e-6
NSLOT = 5
SLABW = 256
FB = 16
NFB = DFF // (FB * 128)


class Op:
    __slots__ = ("eng", "fn", "deps", "kind", "signal", "semkey", "val")

    def __init__(self, eng, fn, deps, kind, semkey):
        self.eng = eng
        self.fn = fn
        self.deps = [d for d in deps if d is not None]
        self.kind = kind
        self.signal = False
        self.semkey = semkey
        self.val = None


class Prog:
    ENGS = ("pe", "act", "dve", "pool", "sp")

    def __init__(self):
        self.ops = {e: [] for e in self.ENGS}
        self.dma_counts = {}

    def add(self, eng, fn, deps=()):
        op = Op(eng, fn, deps, "c", eng)
        for d in op.deps:
            d.signal = True
        self.ops[eng].append(op)
        return op

    def dma(self, queue, fn, semname, deps=()):
        op = Op(queue, fn, deps, "d", "dma_" + semname)
        for d in op.deps:
            d.signal = True
        cnt = self.dma_counts.get(op.semkey, 0) + 16
        self.dma_counts[op.semkey] = cnt
        op.val = cnt
        self.ops[queue].append(op)
        return op

    def finalize(self):
        for e in self.ENGS:
            c = 0
            for op in self.ops[e]:
                if op.kind == "c" and op.signal:
                    c += 1
                    op.val = c
        return sorted(set(self.ENGS) | set(self.dma_counts.keys()))

    def emit(self, eng, e, sems):
        waited = {}
        for op in self.ops[eng]:
            for d in op.deps:
                if d.kind == "c" and d.eng == eng and eng in ("pe", "sp"):
                    continue
                assert d.val is not None
                if waited.get(d.semkey, 0) >= d.val:
                    continue
                e.wait_ge(sems[d.semkey], d.val)
                waited[d.semkey] = d.val
            ins = op.fn(e)
            if op.kind == "d":
                ins.then_inc(sems[op.semkey], 16)
            elif op.signal:
                ins.then_inc(sems[eng], 1)


def build_program(debug=False):
    nc = bass.Bass("TRN2", target_bir_lowering=False)
    P = Prog()

    def dram_in(name, shape):
        return nc.dram_tensor(name, list(shape), F32, kind="ExternalInput").ap()

    xT_d = dram_in("xT", [D, TOK]).rearrange("(k p) t -> p k t", p=128)
    xh_d = dram_in("xh", [D, HALO]).rearrange("(k p) t -> p k t", p=128)
    w_in_d = dram_in("w_in", [D, 3072]).rearrange("(k p) n -> p k n", p=128)
    w_out_d = dram_in("w_out", [D, D]).rearrange("(k p) n -> p k n", p=128)
    w_up_d = dram_in("w_up", [D, DFF]).rearrange("(k p) n -> p k n", p=128)
    w_down_d = dram_in("w_down", [DFF, D]).rearrange("(k p) n -> p k n", p=128)
    wsT_d = dram_in("wsT", [128, 8, 128])
    mask_d = dram_in("mask", [128, 8, 128])
    wpool_d = dram_in("wpool", [128, 4, 2, 256])
    gv_d = dram_in("gvt", [128, 1024])
    bs_d = dram_in("bst", [128, 1024])
    vecs_d = dram_in("vecs", [128, 56])
    invc_d = dram_in("invc", [128, 4, 16])
    out_d = nc.dram_tensor("outT", [D, TOK], F32, kind="ExternalOutput").ap()

    B0 = 16512
    X0 = B0
    H0 = X0 + 66048
    S0 = H0 + 32768
    R0 = S0 + 33024
    C0 = R0 + NSLOT * 8192
    cur = [C0]

    def sb(name, shape, dt, off):
        return nc.alloc_sbuf_tensor_at(name, list(shape), dt, offset=off)

    def calloc(name, shape, dt, nbytes):
        off = cur[0]
        cur[0] += (nbytes + 31) // 32 * 32
        assert cur[0] <= 229376, cur[0]
        return sb(name, shape, dt, off)

    xf = sb("xf", [128, KC, TOK], F32, X0)
    uT = sb("uT", [128, 8, TOK], BF16, X0)
    zE = sb("zE", [128, 8, ZL], F32, X0 + 16384)
    pooled = sb("pooled", [128, 8, TOK], BF16, X0 + 49664)
    xr_mid = sb("xr_mid", [128, 8, TOK], F32, X0 + 16384)
    xr_h = sb("xr_h", [128, 8, TOK], F32, H0)
    hT = sb("hT", [128, KC, TOK], BF16, H0)
    vh = sb("vh", [128, 8, TOK], BF16, S0)
    vf = sb("vf", [128, 2, TOK], F32, S0 + 16384)
    sqv = sb("sqv", [128, 1024], F32, S0 + 24576)
    mtmp = sb("mtmp", [128, 2, 512], F32, S0 + 28672)
    h2T = sb("h2T", [128, KC, TOK], BF16, S0)
    rsr = sb("rsr", [1, TOK], F32, S0 + 16384)
    rs_hi = sb("rs_hi", [1, TOK], BF16, S0 + 20480)
    rs_lo = sb("rs_lo", [1, TOK], BF16, S0 + 22528)
    rs_lo2 = sb("rs_lo2", [1, TOK], BF16, S0 + 24576)
    ringall = sb("ringall", [128, NSLOT, KC, SLABW], BF16, R0)

    gv_t = calloc("gv_t", [128, 1024], F32, 4096)
    rtmp = sb("rtmp", [128, 2, 512], F32, C0)
    bs_t = calloc("bs_t", [128, 1024], F32, 4096)
    rstd_bc = calloc("rstd_bc", [128, TOK], F32, 4096)
    wpool = calloc("wpool", [128, 4, 2, 256], BF16, 4096)
    wsT = calloc("wsT", [128, 8, 128], BF16, 2048)
    maskt = calloc("maskt", [128, 8, 128], BF16, 2048)
    sqr = calloc("sqr", [128, 2, TOK], BF16, 4096)
    xst_off = cur[0]
    xstage = [calloc("xstage%d" % i, [128, TOK], F32, 4160) for i in range(2)]
    tmpA = sb("tmpA", [128, ZL], F32, xst_off)
    tmpB = sb("tmpB", [128, ZL], F32, xst_off + 4160)
    ones = calloc("ones", [128, 128], BF16, 256)
    vecs = calloc("vecs", [128, 56], F32, 224)
    invc = calloc("invc", [128, 4, 16], F32, 256)
    xh = calloc("xh", [128, KC, HALO], F32, 1024)
    hh = calloc("hh", [128, KC, HALO], BF16, 512)
    sqh = calloc("sqh", [128, KC, HALO], BF16, 512)
    rstd_h = calloc("rstd_h", [128, HALO], F32, 64)
    vstat = calloc("vstat", [128, 2, 8], F32, 64)
    vrs = calloc("vrs", [128, 2, 8], F32, 64)
    ptmp = calloc("ptmp", [128, 16], F32, 64)
    epst = calloc("epst", [128, 1], F32, 32)
    onef = calloc("onef", [128, 1], F32, 32)
    rstd_col = calloc("rstd_col", [128, 8], F32, 32)

    G_MIX, G_FFN, G_FIN, PSC = 0, 16, 32, 48

    def MIX(k):
        return uT[:, k, :] if k < 8 else pooled[:, k - 8, :]

    def XR(m):
        if 4 <= m < 12:
            return xr_mid[:, m - 4, :]
        return xr_h[:, m if m < 4 else m - 8, :]

    def RS(n, k, a, b):
        return ringall[:, n % NSLOT, k, a:b]

    ps = nc.alloc_psum_tensor("ps", [128, 8, 512], F32)
    NRING = 6
    bank_last = [None] * 8
    bank_next = [0]

    def take_bank():
        b = bank_next[0]
        bank_next[0] = (b + 1) % NRING
        return b

    def pe_job(segments, nb=2):
        banks = [take_bank() for _ in range(nb)]
        first = True
        last = None
        for seg in segments:
            if callable(seg):
                seg()
                continue
            mms, deps = seg
            d = list(deps() if callable(deps) else deps)
            if first:
                d += [bank_last[b] for b in banks]
                first = False

            def fn(e, mms=mms):
                ins = None
                for (sel, c0, c1, lhsT, rhs, st, sp_) in mms:
                    ins = e.matmul(ps[:, banks[sel], c0:c1], lhsT, rhs, start=st, stop=sp_)
                return ins
            last = P.add("pe", fn, d)
        return last, banks

    def proj_mms(lhs_fn, rhs_fn, ks, k_first, k_last):
        mms = []
        for k in ks:
            mms.append((0, 0, 512, lhs_fn(k), rhs_fn(k)[:, 0:512], k == k_first, k == k_last))
            mms.append((1, 0, 512, lhs_fn(k), rhs_fn(k)[:, 512:1024], k == k_first, k == k_last))
        return mms

    vecs_ld = P.dma("sp", lambda e: e.dma_start(out=vecs[:, :], in_=vecs_d[:, :]), "cst0")
    slab0a = [P.dma("pool", lambda e: e.dma_start(out=ringall[:, 0, :, 0:128], in_=w_in_d[:, :, 2048:2048 + 128]), "slot0a")]
    x_ld = []
    for q in range(4):
        x_ld.append(P.dma("sp", lambda e, q=q: e.dma_start(out=xf[:, 4 * q:4 * q + 4, :], in_=xT_d[:, 4 * q:4 * q + 4, :]),
                          "x%d" % q, [slab0a[0]]))
    cst1 = None
    for fn in (lambda e: e.dma_start(out=xh[:, :, :], in_=xh_d[:, :, :]),
               lambda e: e.dma_start(out=invc[:, :, :], in_=invc_d[:, :, :])):
        cst1 = P.dma("sp", fn, "cst1")

    slabs = []
    for j in range(4):
        slabs.append(w_in_d[:, :, 2048 + SLABW * j:2048 + SLABW * (j + 1)])
    for j in range(4):
        slabs.append(w_in_d[:, :, SLABW * j:SLABW * (j + 1)])
    for j in range(4):
        slabs.append(w_in_d[:, :, 1024 + SLABW * j:1024 + SLABW * (j + 1)])
    for _rep in range(2):
        for j in range(8):
            slabs.append(w_out_d[:, :, SLABW * j:SLABW * (j + 1)])
    for fb in range(NFB):
        for j in range(8):
            c0 = fb * FB * 128 + SLABW * j
            slabs.append(w_up_d[:, :, c0:c0 + SLABW])
        for j in range(8):
            slabs.append(w_down_d[:, fb * FB:(fb + 1) * FB, SLABW * j:SLABW * (j + 1)])
    for j in range(8):
        slabs.append(w_down_d[:, (NFB - 1) * FB:NFB * FB, SLABW * j:SLABW * (j + 1)])
    NSLAB = len(slabs)
    slab_ld = [None] * NSLAB
    slab_done = [None] * NSLAB
    nxt = [0]

    def issue_loads(upto):
        upto = min(upto, NSLAB - 1)
        while nxt[0] <= upto:
            n = nxt[0]
            deps = []
            if n >= NSLOT:
                assert slab_done[n - NSLOT] is not None, n
                deps.append(slab_done[n - NSLOT])
            else:
                deps.append(x_ld[3])
            if n == 0:
                slab_ld[0] = P.dma("pool", lambda e: e.dma_start(out=ringall[:, 0, :, 128:256], in_=slabs[0][:, :, 128:256]), "slot0", deps)
            else:
                slab_ld[n] = P.dma("pool", lambda e, n=n: e.dma_start(out=ringall[:, n % NSLOT, :, :], in_=slabs[n]),
                                   "slot%d" % (n % NSLOT), deps)
            nxt[0] += 1

    issue_loads(NSLOT - 1)
    cstp = None
    for fn in (lambda e: e.dma_start(out=wsT[:, :, :], in_=wsT_d[:, :, :]),
               lambda e: e.dma_start(out=maskt[:, :, :], in_=mask_d[:, :, :]),
               lambda e: e.dma_start(out=wpool[:, :, :, :], in_=wpool_d[:, :, :, :])):
        cstp = P.dma("pool", fn, "cstp", [x_ld[3]])
    cst = None
    for fn in (lambda e: e.dma_start(out=gv_t[:, :], in_=gv_d[:, :]),
               lambda e: e.dma_start(out=bs_t[:, :], in_=bs_d[:, :])):
        cst = P.dma("sp", fn, "cst", [slab_ld[NSLOT - 1]])

    ones_op = P.add("dve", lambda e: e.memset(ones[:, :], 1.0))
    eps_op = P.add("dve", lambda e: e.memset(epst[:, :], EPS))
    onef_op = P.add("dve", lambda e: e.memset(onef[:, :], 1.0))
    EPSB = epst[:, 0:1]

    sq_reader = [None, None]

    def stat_chunk(k, src_ap, src_deps, eng):
        s = k % 2
        if eng == "act":
            sq = P.add("act", lambda e: e.activation(out=sqr[:, s, :], in_=src_ap, func=AF.Square), list(src_deps) + [sq_reader[s]])
        else:
            sq = P.add("dve", lambda e: e.tensor_tensor(out=sqr[:, s, :], in0=src_ap, in1=src_ap, op=ALU.mult), list(src_deps) + [sq_reader[s]])

        def mk():
            def fn(e):
                e.matmul(ps[:, 6, :], ones[:, :], sqr[:, s, 0:512], start=(k == 0), stop=(k == KC - 1))
                return e.matmul(ps[:, 7, :], ones[:, :], sqr[:, s, 512:1024], start=(k == 0), stop=(k == KC - 1))
            d = [sq, ones_op]
            if k == 0:
                d += [bank_last[6], bank_last[7]]
            pe = P.add("pe", fn, d)
            sq_reader[s] = pe
            return pe
        return sq, mk

    def rstd_chain(pe_op, war_deps):
        s1 = P.add("act", lambda e: e.activation(out=rstd_bc[:, 0:512], in_=ps[:, 6, :], func=AF.Ln, bias=EPSB, scale=1.0 / D),
                   [pe_op, eps_op] + list(war_deps))
        s2 = P.add("act", lambda e: e.activation(out=rstd_bc[:, 512:1024], in_=ps[:, 7, :], func=AF.Ln, bias=EPSB, scale=1.0 / D),
                   [pe_op, eps_op] + list(war_deps))
        bank_last[6] = s1
        bank_last[7] = s2
        P.add("act", lambda e: e.activation(out=rstd_bc[:, 0:512], in_=rstd_bc[:, 0:512], func=AF.Exp, scale=-0.5), [s1])
        return P.add("act", lambda e: e.activation(out=rstd_bc[:, 512:1024], in_=rstd_bc[:, 512:1024], func=AF.Exp, scale=-0.5), [s2])

    h_ops = [None] * KC
    n1_mm = [None]
    z_evac = [None] * 8
    zh_last = [None]
    last_rstd_reader = [None]

    def z_evacs(m, job, banks, hjob):
        e1 = P.add("dve", lambda e: e.tensor_tensor(out=zE[:, m, HALO:HALO + 512], in0=ps[:, banks[0], :], in1=rstd_bc[:, 0:512], op=ALU.mult), [job, r1[0]])
        e2 = P.add("dve", lambda e: e.tensor_tensor(out=zE[:, m, HALO + 512:ZL], in0=ps[:, banks[1], :], in1=rstd_bc[:, 512:1024], op=ALU.mult), [job, r1[0]])
        e3 = P.add("dve", lambda e: e.tensor_tensor(out=zE[:, m, 0:HALO], in0=ps[:, 7, m * HALO:(m + 1) * HALO], in1=rstd_h[:, :], op=ALU.mult), [hjob, hs2])
        bank_last[banks[0]] = e1
        bank_last[banks[1]] = e2
        bank_last[7] = e3
        z_evac[m] = (e1, e2, e3)
        last_rstd_reader[0] = e2

    def z_halo_job(m, n, mmi):
        def halo_fn(e):
            ins = None
            for k in range(KC):
                ins = e.matmul(ps[:, 7, m * HALO:(m + 1) * HALO], RS(n, k, mmi * 128, (mmi + 1) * 128), hh[:, k, :],
                               start=(k == 0), stop=(k == KC - 1))
            return ins
        return P.add("pe", halo_fn, [slab_ld[n] if (n, mmi) != (0, 0) else slab0a[0], hh_all, bank_last[7]])

    segs = []
    for k in range(KC):
        def hook(k=k):
            xk = xf[:, k, :]
            ld = x_ld[k // 4]
            if k % 2 == 0:
                h_ops[k] = P.add("act", lambda e: e.activation(out=hT[:, k, :], in_=xk, func=AF.Copy, scale=vecs[:, G_MIX + k:G_MIX + k + 1]), [ld, vecs_ld])
                sq, mk = stat_chunk(k, xk, [ld], "dve")
            else:
                h_ops[k] = P.add("dve", lambda e: e.tensor_scalar(out=hT[:, k, :], in0=xk, scalar1=vecs[:, G_MIX + k:G_MIX + k + 1], scalar2=None, op0=ALU.mult), [ld, vecs_ld])
                sq, mk = stat_chunk(k, xk, [ld], "act")
            n1_mm[0] = mk()
        segs.append(hook)
        segs.append((proj_mms(lambda kk: RS(0, kk, 0, 128), lambda kk: hT[:, kk, :], [k], 0, KC - 1), lambda k=k: [slab0a[0], h_ops[k]]))
    zjob0, zbanks0 = pe_job(segs)
    r1 = [rstd_chain(n1_mm[0], [])]

    sqh_op = P.add("dve", lambda e: e.tensor_tensor(out=sqh[:, :, :], in0=xh[:, :, :], in1=xh[:, :, :], op=ALU.mult), [cst1])
    bh = take_bank()

    def halo_stat(e):
        ins = None
        for k in range(KC):
            ins = e.matmul(ps[:, bh, 0:HALO], ones[:, :], sqh[:, k, :], start=(k == 0), stop=(k == KC - 1))
        return ins
    hs_pe = P.add("pe", halo_stat, [sqh_op, ones_op, bank_last[bh]])
    hs1 = P.add("act", lambda e: e.activation(out=rstd_h[:, :], in_=ps[:, bh, 0:HALO], func=AF.Ln, bias=EPSB, scale=1.0 / D), [hs_pe, eps_op])
    bank_last[bh] = hs1
    hs2 = P.add("act", lambda e: e.activation(out=rstd_h[:, :], in_=rstd_h[:, :], func=AF.Exp, scale=-0.5), [hs1])

    h_all = [h_ops[KC - 1], h_ops[KC - 2]]

    hh_ops = []
    for k in range(KC):
        hh_ops.append(P.add("dve", lambda e, k=k: e.tensor_scalar(out=hh[:, k, :], in0=xh[:, k, :], scalar1=vecs[:, G_MIX + k:G_MIX + k + 1],
                                                                   scalar2=None, op0=ALU.mult), [cst1, vecs_ld]))
    hh_all = hh_ops[-1]

    pool_ops = [None] * 8
    prev_tmp_user = [None]

    def pooling_chunk(c):
        g = c // 2
        win = 2 << g
        E = zE[:, c, :]
        deps0 = list(z_evac[c]) + [prev_tmp_user[0]]
        o = P.add("dve", lambda e: e.tensor_tensor(out=tmpA[:, 1:ZL], in0=E[:, 1:ZL], in1=E[:, 0:ZL - 1], op=ALU.add), deps0)
        src, dst = tmpA, tmpB
        sh, lo = 2, 1
        while sh < win:
            lo2 = lo + sh
            o = P.add("dve", lambda e, src=src, dst=dst, sh=sh, lo2=lo2: e.tensor_tensor(
                out=dst[:, lo2:ZL], in0=src[:, lo2:ZL], in1=src[:, lo2 - sh:ZL - sh], op=ALU.add), [o])
            src, dst = dst, src
            lo = lo2
            sh *= 2
        assert lo <= HALO
        fin = P.add("dve", lambda e, src=src: e.scalar_tensor_tensor(
            out=pooled[:, c, :], in0=src[:, HALO:ZL], scalar=1.0 / win, in1=E[:, HALO:ZL], op0=ALU.mult, op1=ALU.subtract), [o])
        f1 = P.add("dve", lambda e, src=src: e.tensor_tensor(out=ptmp[:, :], in0=src[:, HALO:2 * HALO], in1=invc[:, g, :], op=ALU.mult), [fin, cst1])
        f2 = P.add("dve", lambda e: e.tensor_tensor(out=pooled[:, c, 0:HALO], in0=ptmp[:, :], in1=E[:, HALO:2 * HALO], op=ALU.subtract), [f1])
        prev_tmp_user[0] = f2
        pool_ops[c] = f2

    c1 = P.add("dve", lambda e: e.tensor_copy(out=rs_hi[:, :], in_=rstd_bc[0:1, :]), [r1[0]])
    c2 = P.add("dve", lambda e: e.tensor_tensor(out=rsr[:, :], in0=rstd_bc[0:1, :], in1=rs_hi[:, :], op=ALU.subtract), [c1])
    c3 = P.add("dve", lambda e: e.tensor_copy(out=rs_lo[:, :], in_=rsr[:, :]), [c2])
    c4 = P.add("dve", lambda e: e.tensor_tensor(out=rsr[:, :], in0=rsr[:, :], in1=rs_lo[:, :], op=ALU.subtract), [c3])
    c5 = P.add("dve", lambda e: e.tensor_copy(out=rs_lo2[:, :], in_=rsr[:, :]), [c4])
    hj = z_halo_job(0, 0, 0)
    z_evacs(0, zjob0, zbanks0, hj)
    rc_evac = None
    for m in range(1, 8):
        n, mmi = m // 2, m % 2
        if mmi == 0:
            issue_loads(n + NSLOT - 1)
        job, banks = pe_job([(proj_mms(lambda k: RS(n, k, mmi * 128, (mmi + 1) * 128), lambda k: hT[:, k, :], range(KC), 0, KC - 1),
                              [slab_ld[n]] + h_all)])
        hj = z_halo_job(m, n, mmi)
        slab_done[n] = hj
        z_evacs(m, job, banks, hj)
        if m % 2 == 1:
            pooling_chunk(m // 2)
        if m == 1:
            slab_done[0] = hj
        if m == 3:
            bc = take_bank()

            def col_fn(e, bc=bc):
                ins = None
                first = True
                for i in range(8):
                    for t in (rs_hi, rs_lo, rs_lo2):
                        ins = e.matmul(ps[:, bc, i:i + 1], t[0:1, i * 128:(i + 1) * 128], ones[0:1, 0:1], start=first, stop=(i == 7 and t is rs_lo2))
                        first = False
                return ins
            col_pe = P.add("pe", col_fn, [c1, c3, c5, ones_op, bank_last[bc]])
            rc_evac = P.add("act", lambda e, bc=bc: e.activation(out=rstd_col[:, :], in_=ps[:, bc, 0:8], func=AF.Copy), [col_pe])
            bank_last[bc] = rc_evac

    ut_reader = [None, None]
    ut_ctr = [0]
    u_evac_last = None
    for m in range(8):
        n, mmi = 4 + m // 2, m % 2
        if mmi == 0:
            issue_loads(n + NSLOT - 1)
        job, banks = pe_job([(proj_mms(lambda k: RS(n, k, mmi * 128, (mmi + 1) * 128), lambda k: hT[:, k, :], range(KC), 0, KC - 1),
                              [slab_ld[n]] + h_all)])
        slab_done[n] = job
        for half in range(2):
            s = ut_ctr[0] % 2
            ut_ctr[0] += 1
            a1 = P.add("dve", lambda e, s=s, half=half, b=banks[half]: e.tensor_tensor(
                out=mtmp[:, s, :], in0=ps[:, b, :], in1=rstd_bc[:, half * 512:(half + 1) * 512], op=ALU.mult), [job, r1[0], ut_reader[s]])
            bank_last[banks[half]] = a1
            a2 = P.add("act", lambda e, s=s, half=half, m=m: e.activation(out=uT[:, m, half * 512:(half + 1) * 512], in_=mtmp[:, s, :], func=AF.Gelu_apprx_tanh), [a1])
            ut_reader[s] = a2
            u_evac_last = a2
            last_rstd_reader[0] = a1
        if m % 2 == 0:
            pooling_chunk(4 + m // 2)

    pool_all = pool_ops[7]

    ws_op = P.add("dve", lambda e: e.tensor_tensor(out=wsT[:, :, :], in0=wsT[:, :, :], in1=maskt[:, :, :], op=ALU.mult), [cstp])
    issue_loads(12)
    v_deps = [slab_ld[8 + j] for j in range(4)]
    vf_readers = [None, None]
    vst_readers = [None, None]
    vh_ops = [None] * 8
    v_jobs = []
    mix_last = [None]
    mix_tile_last = [None] * 8
    mt_readers = [ut_reader[0], ut_reader[1]]
    mt_ctr = [0]

    def emit_mixing(i):
        for hb in range(2):
            ba = take_bank()

            def fn(e, ba=ba, hb=hb):
                ins = None
                for hq in range(4):
                    h = 4 * hb + hq
                    ins = e.matmul(ps[:, ba, hq * 128:(hq + 1) * 128], vh[:, i, h * 128:(h + 1) * 128], wsT[:, h, :], start=True, stop=True)
                return ins
            job = P.add("pe", fn, [vh_ops[i], ws_op, bank_last[ba]])
            s = mt_ctr[0] % 2
            mt_ctr[0] += 1
            a1 = P.add("dve", lambda e, s=s, ba=ba, hb=hb: e.tensor_tensor(
                out=mtmp[:, s, :], in0=ps[:, ba, :], in1=bs_t[:, 512 * hb:512 * hb + 512], op=ALU.add), [job, cst, mt_readers[s]])
            bank_last[ba] = a1
            a2 = P.add("dve", lambda e, s=s, hb=hb: e.tensor_tensor(
                out=uT[:, 4 * hb:4 * hb + 4, i * 128:(i + 1) * 128],
                in0=mtmp[:, s, :].rearrange("p (h t) -> p h t", h=4),
                in1=uT[:, 4 * hb:4 * hb + 4, i * 128:(i + 1) * 128], op=ALU.mult), [a1, u_evac_last])
            mt_readers[s] = a2
            mix_last[0] = a2
            mix_tile_last[i] = a2

    ob_last = [None]

    def emit_pool_mm():
        for g in range(4):
            jobs = []
            for mmi in range(2):
                mms = []
                for kc in range(2):
                    lhsT = wpool[:, g, kc, mmi * 128:(mmi + 1) * 128]
                    mms.append((0, 0, 512, lhsT, pooled[:, 2 * g + kc, 0:512], kc == 0, kc == 1))
                    mms.append((1, 0, 512, lhsT, pooled[:, 2 * g + kc, 512:1024], kc == 0, kc == 1))
                jobs.append(pe_job([(mms, [cstp, pool_ops[2 * g], pool_ops[2 * g + 1]])]))
            for mmi in range(2):
                job, banks = jobs[mmi]
                c = 2 * g + mmi
                for half in range(2):
                    ev = P.add("act", lambda e, c=c, half=half, b=banks[half]: e.activation(
                        out=pooled[:, c, half * 512:(half + 1) * 512], in_=ps[:, b, :], func=AF.Copy, scale=vecs[:, PSC + c:PSC + c + 1]),
                        [jobs[0][0], jobs[1][0], cst])
                    bank_last[banks[half]] = ev
                    ob_last[0] = ev

    for i in range(8):
        mms = []
        for k in range(KC):
            lhsT = hT[:, k, i * 128:(i + 1) * 128]
            mms.append((0, 0, 512, lhsT, ringall[:, 3:5, k, :], k == 0, k == KC - 1))
            mms.append((1, 0, 512, lhsT, ringall[:, 0:2, k, :], k == 0, k == KC - 1))
        job, banks = pe_job([(mms, v_deps + h_all)])
        v_jobs.append(job)
        s = i % 2
        g1 = P.add("act", lambda e, s=s, i=i, b=banks[0]: e.activation(out=vf[:, s, 0:512], in_=ps[:, b, :], func=AF.Gelu_apprx_tanh,
                                                                        scale=rstd_col[:, i:i + 1]), [job, rc_evac, vf_readers[s]])
        g2 = P.add("act", lambda e, s=s, i=i, b=banks[1]: e.activation(out=vf[:, s, 512:1024], in_=ps[:, b, :], func=AF.Gelu_apprx_tanh,
                                                                        scale=rstd_col[:, i:i + 1]), [job, rc_evac, vf_readers[s]])
        bank_last[banks[0]] = g1
        bank_last[banks[1]] = g2
        q2 = None
        for h in range(8):
            q2 = P.add("act", lambda e, s=s, h=h: e.activation(out=sqv[:, h * 128:(h + 1) * 128], in_=vf[:, s, h * 128:(h + 1) * 128], func=AF.Square,
                                                              accum_out=vstat[:, s, h:h + 1]), [g1, g2, vst_readers[s]])
        q3 = P.add("act", lambda e, s=s: e.activation(out=vrs[:, s, :], in_=vstat[:, s, :], func=AF.Ln, bias=EPSB, scale=1.0 / 128), [q2, eps_op])
        q4 = P.add("act", lambda e, s=s: e.activation(out=vrs[:, s, :], in_=vrs[:, s, :], func=AF.Exp, scale=-0.5), [q3])
        if i >= 2:
            emit_mixing(i - 2)
        if i == 7:
            emit_mixing(6)
        last = None
        for h in range(8):
            last = P.add("dve", lambda e, s=s, h=h, i=i: e.scalar_tensor_tensor(
                out=vh[:, i, h * 128:(h + 1) * 128], in0=vf[:, s, h * 128:(h + 1) * 128], scalar=vrs[:, s, h:h + 1],
                in1=gv_t[:, h * 128:(h + 1) * 128], op0=ALU.mult, op1=ALU.mult), [q4, cst])
        vf_readers[s] = last
        vst_readers[s] = last
        vh_ops[i] = last
    for j in range(4):
        slab_done[8 + j] = v_jobs[-1]
    h_dead = v_jobs[-1]
    emit_pool_mm()

    xs4 = [xstage[0][:, 0:512], xstage[0][:, 512:1024], xstage[1][:, 0:512], xstage[1][:, 512:1024]]
    xs_reader4 = [pool_all] * 4
    order = [(half, m) for half in range(2) for m in range(KC)]
    xs_ld = {}

    def load_xs(idx):
        half, m = order[idx]
        s = idx % 4
        xs_ld[idx] = P.dma("sp", lambda e: e.dma_start(out=xs4[s], in_=xT_d[:, m, half * 512:(half + 1) * 512]), "xs%d" % s, [xs_reader4[s]])

    for idx in range(4):
        load_xs(idx)
    x1_e = [[None] * KC, [None] * KC]
    h2h = [[None] * KC, [None] * KC]
    hs_reader = [sq_reader[0], sq_reader[0], sq_reader[1], sq_reader[1]]
    hs_ctr = [0]
    pend = []
    rexp2 = [None, None]
    h2_defer = []
    wo_last = None
    korder = list(range(8, 16)) + list(range(8))

    def flush_one():
        pe, half, m = pend.pop(0)()
        if m == KC - 1:
            c0, c1 = half * 512, (half + 1) * 512
            ln = P.add("act", lambda e: e.activation(out=rstd_bc[:, c0:c1], in_=ps[:, 6 + half, :], func=AF.Ln, bias=EPSB, scale=1.0 / D),
                       [pe, eps_op, last_rstd_reader[0]])
            bank_last[6 + half] = ln
            rexp2[half] = P.add("act", lambda e: e.activation(out=rstd_bc[:, c0:c1], in_=rstd_bc[:, c0:c1], func=AF.Exp, scale=-0.5), [ln])

    for idx, (half, m) in enumerate(order):
        c0, c1 = half * 512, (half + 1) * 512
        n, mmi = 12 + half * 8 + m // 2, m % 2
        if mmi == 0:
            issue_loads(n + NSLOT - 1)
        mms = [(0, 0, 512, RS(n, k, mmi * 128, (mmi + 1) * 128), MIX(k)[:, c0:c1], k == korder[0], k == korder[-1]) for k in korder]
        job, banks = pe_job([(mms, [slab_ld[n], ob_last[0], mix_tile_last[3 if half == 0 else 7]])], nb=1)
        slab_done[n] = job
        wo_last = job
        if idx == 2:
            emit_mixing(7)
            for f_ in h2_defer:
                f_()
            h2_defer = None
        while len(pend) > (3 if half == 0 else 1):
            flush_one()
        ev = P.add("dve", lambda e, m=m, b=banks[0], c0=c0, c1=c1, xs=xs4[idx % 4]: e.tensor_tensor(out=XR(m)[:, c0:c1], in0=ps[:, b, :], in1=xs, op=ALU.add),
                   [job, xs_ld[idx], h_dead, pool_all])
        bank_last[banks[0]] = ev
        x1_e[half][m] = ev
        xs_reader4[idx % 4] = ev
        if idx + 4 < len(order):
            load_xs(idx + 4)
        hsl = hs_ctr[0] % 4
        hs_ctr[0] += 1
        sq_ap = sqr[:, hsl // 2, (hsl % 2) * 512:(hsl % 2) * 512 + 512]
        sq = P.add("act", lambda e, m=m, sq_ap=sq_ap, c0=c0, c1=c1: e.activation(out=sq_ap, in_=XR(m)[:, c0:c1], func=AF.Square), [ev, hs_reader[hsl]])

        def mk(m=m, half=half, sq=sq, sq_ap=sq_ap, hsl=hsl):
            d = [sq, ones_op]
            if m == 0:
                d.append(bank_last[6 + half])
            pe = P.add("pe", lambda e: e.matmul(ps[:, 6 + half, :], ones[:, :], sq_ap, start=(m == 0), stop=(m == KC - 1)), d)
            hs_reader[hsl] = pe
            return pe, half, m
        pend.append(mk)

        def mk_h2(m=m, half=half, ev=ev, c0=c0, c1=c1):
            h2h[half][m] = P.add("act", lambda e: e.activation(out=h2T[:, m, c0:c1], in_=XR(m)[:, c0:c1], func=AF.Copy,
                                                              scale=vecs[:, G_FFN + m:G_FFN + m + 1]), [ev, mix_tile_last[7], cst])
        if h2_defer is not None:
            h2_defer.append(mk_h2)
        else:
            mk_h2()
    x1_ops = x1_e[1]
    h2_ops = h2h[1]

    rt_readers = [None, None]
    rt_ctr = [0]
    x2_ops = list(x1_ops)
    x2_e1 = list(x1_e[0])
    nbase = 28
    r2 = [None]
    prev_down_last = None
    fin_mk = [None] * KC
    n3_mm = [None]
    for fb in range(NFB):
        act_ops = [None] * FB
        lastblk = fb == NFB - 1
        for f in range(FB):
            n, mmi = nbase + fb * 16 + f // 2, f % 2
            if mmi == 0:
                issue_loads(n + NSLOT - 1)
            lhs = lambda k: RS(n, k, mmi * 128, (mmi + 1) * 128)
            rhs = lambda k: h2T[:, k, :]
            if fb == 0 and f == 0:
                segs = []
                for k in range(KC):
                    if k == KC - 1:
                        def hook():
                            while pend:
                                flush_one()
                        segs.append(hook)
                    segs.append((proj_mms(lhs, rhs, [k], 0, KC - 1), [slab_ld[n], h2_ops[k], wo_last]))
                job, banks = pe_job(segs)
                r2[0] = rexp2[1]
            else:
                job, banks = pe_job([(proj_mms(lhs, rhs, range(KC), 0, KC - 1), [slab_ld[n], h2_ops[KC - 1]])])
            slab_done[n] = job
            last = None
            for half in range(2):
                s = rt_ctr[0] % 2
                rt_ctr[0] += 1
                a1 = P.add("dve", lambda e, s=s, half=half, b=banks[half]: e.scalar_tensor_tensor(
                    out=rtmp[:, s, :], in0=ps[:, b, :], scalar=0.0, in1=rstd_bc[:, half * 512:(half + 1) * 512], op0=ALU.max, op1=ALU.mult),
                    [job, r2[0], rt_readers[s], vh_ops[7]])
                bank_last[banks[half]] = a1
                last_rstd_reader[0] = a1
                a2 = P.add("act", lambda e, s=s, f=f, half=half: e.activation(out=MIX(f)[:, half * 512:(half + 1) * 512], in_=rtmp[:, s, :], func=AF.Square),
                           [a1, wo_last, prev_down_last])
                rt_readers[s] = a2
                last = a2
            act_ops[f] = last
        if lastblk:
            break
        dj = None
        for m in range(KC):
            n, mmi = nbase + fb * 16 + 8 + m // 2, m % 2
            if mmi == 0:
                issue_loads(n + NSLOT - 1)
            lhs = lambda k: RS(n, k, mmi * 128, (mmi + 1) * 128)
            if m == 0:
                segs = [(proj_mms(lhs, MIX, [k], 0, FB - 1), [slab_ld[n], act_ops[k]]) for k in range(FB)]
            else:
                segs = [(proj_mms(lhs, MIX, range(FB), 0, FB - 1), [slab_ld[n], act_ops[FB - 1]])]
            job, banks = pe_job(segs)
            slab_done[n] = job
            dj = job
            e1 = P.add("dve", lambda e, m=m, b=banks[0]: e.tensor_tensor(out=XR(m)[:, 0:512], in0=XR(m)[:, 0:512], in1=ps[:, b, :], op=ALU.add),
                       [job, x2_ops[m], x2_e1[m], h2_ops[m]])
            e2 = P.add("dve", lambda e, m=m, b=banks[1]: e.tensor_tensor(out=XR(m)[:, 512:1024], in0=XR(m)[:, 512:1024], in1=ps[:, b, :], op=ALU.add),
                       [job, x2_ops[m], x2_e1[m], h2_ops[m]])
            bank_last[banks[0]] = e1
            bank_last[banks[1]] = e2
            x2_e1[m] = e1
            x2_ops[m] = e2
        prev_down_last = dj

    fb = NFB - 1
    pre_ops = [[None] * KC, [None] * KC]
    rexp = [None, None]

    def final_half_out(k, half):
        c0, c1 = half * 512, (half + 1) * 512
        o = P.add("dve", lambda e: e.tensor_tensor(out=XR(k)[:, c0:c1], in0=XR(k)[:, c0:c1], in1=rstd_bc[:, c0:c1], op=ALU.mult),
                  [rexp[half], pre_ops[half][k]])
        P.dma("sp", lambda e: e.dma_start(out=out_d[k * 128:(k + 1) * 128, c0:c1], in_=XR(k)[:, c0:c1]), "st", [o])

    for half in range(2):
        c0, c1 = half * 512, (half + 1) * 512
        pending = [None]
        last_mm = None
        for m in range(KC):
            n, mmi = nbase + fb * 16 + 8 + half * 8 + m // 2, m % 2
            if mmi == 0:
                issue_loads(n + NSLOT - 1)
            if half == 0 and m == 0:
                segs = [([(0, 0, 512, RS(n, k, mmi * 128, (mmi + 1) * 128), MIX(k)[:, c0:c1], k == 0, k == FB - 1)], [slab_ld[n], act_ops[k]])
                        for k in range(FB)]
            else:
                segs = [([(0, 0, 512, RS(n, k, mmi * 128, (mmi + 1) * 128), MIX(k)[:, c0:c1], k == 0, k == FB - 1) for k in range(FB)],
                         [slab_ld[n], act_ops[FB - 1]])]
            job, banks = pe_job(segs, nb=1)
            slab_done[n] = job
            if pending[0] is not None:
                last_mm = pending[0]()
            ev = P.add("dve", lambda e, m=m, b=banks[0], c0=c0, c1=c1: e.tensor_tensor(out=XR(m)[:, c0:c1], in0=XR(m)[:, c0:c1], in1=ps[:, b, :], op=ALU.add),
                       [job, x2_ops[m], x2_e1[m], h2_ops[m]])
            bank_last[banks[0]] = ev
            hsl = hs_ctr[0] % 4
            hs_ctr[0] += 1
            sq_ap = sqr[:, hsl // 2, (hsl % 2) * 512:(hsl % 2) * 512 + 512]
            sq = P.add("act", lambda e, m=m, sq_ap=sq_ap, c0=c0, c1=c1: e.activation(out=sq_ap, in_=XR(m)[:, c0:c1], func=AF.Square), [ev, hs_reader[hsl]])

            def mk(m=m, sq=sq, sq_ap=sq_ap, hsl=hsl, half=half):
                d = [sq, ones_op]
                if m == 0:
                    d.append(bank_last[6 + half])
                pe = P.add("pe", lambda e: e.matmul(ps[:, 6 + half, :], ones[:, :], sq_ap, start=(m == 0), stop=(m == KC - 1)), d)
                hs_reader[hsl] = pe
                return pe
            pending[0] = mk
            pre_ops[half][m] = P.add("act", lambda e, m=m, c0=c0, c1=c1: e.activation(out=XR(m)[:, c0:c1], in_=XR(m)[:, c0:c1], func=AF.Copy,
                                                                                      scale=vecs[:, G_FIN + m:G_FIN + m + 1]), [ev, sq, cst])
            if half == 1:
                final_half_out(m, 0)
        last_mm = pending[0]()
        ln = P.add("act", lambda e, half=half, c0=c0, c1=c1: e.activation(out=rstd_bc[:, c0:c1], in_=ps[:, 6 + half, :], func=AF.Ln, bias=EPSB, scale=1.0 / D),
                   [last_mm, eps_op, last_rstd_reader[0]])
        bank_last[6 + half] = ln
        rexp[half] = P.add("act", lambda e, c0=c0, c1=c1: e.activation(out=rstd_bc[:, c0:c1], in_=rstd_bc[:, c0:c1], func=AF.Exp, scale=-0.5), [ln])
    for k in range(KC):
        final_half_out(k, 1)

    keys = P.finalize()
    with ExitStack() as es:
        sems = {k: es.enter_context(nc.semaphore("s_" + k)) for k in keys}
        block = es.enter_context(nc.Block())

        @block.tensor
        def _(e):
            P.emit("pe", e, sems)

        @block.scalar
        def _(e):
            P.emit("act", e, sems)

        @block.vector
        def _(e):
            P.emit("dve", e, sems)

        @block.gpsimd
        def _(e):
            P.emit("pool", e, sems)

        @block.sync
        def _(e):
            P.emit("sp", e, sems)
            e.wait_ge(sems["dma_st"], P.dma_counts["dma_st"])
    return nc


_CACHE = {}


def _host_inputs(x, g_mix, w_in, g_v, w_s, b_s, w_pool, pool_scale, w_out, g_ffn, w_up, w_down, g_final):
    f32 = np.float32
    x = np.asarray(x, f32)
    w_in2 = np.ascontiguousarray(np.asarray(w_in, f32)[0])
    w_out2 = np.ascontiguousarray(np.asarray(w_out, f32)[0])
    w_up2 = np.ascontiguousarray(np.asarray(w_up, f32)[0])
    w_down2 = np.ascontiguousarray(np.asarray(w_down, f32)[0])
    wsT = np.ascontiguousarray(np.asarray(w_s, f32)[0].transpose(2, 0, 1))
    s_idx = np.arange(128)[:, None, None]
    t_idx = np.arange(128)[None, None, :]
    mask = np.ascontiguousarray(np.broadcast_to((s_idx <= t_idx), (128, 8, 128)).astype(f32))
    wpool = np.ascontiguousarray(np.asarray(w_pool, f32)[0].reshape(4, 2, 128, 256).transpose(2, 0, 1, 3))
    gvt = np.ascontiguousarray(np.broadcast_to(np.asarray(g_v, f32)[0][None, :], (128, 1024)))
    bst = np.ascontiguousarray(np.broadcast_to(np.asarray(b_s, f32)[0].reshape(1, 1024), (128, 1024)))
    vecs = np.zeros((128, 56), f32)
    vecs[:, 0:16] = np.asarray(g_mix, f32)[0].reshape(16, 128).T
    vecs[:, 16:32] = np.asarray(g_ffn, f32)[0].reshape(16, 128).T
    vecs[:, 32:48] = np.asarray(g_final, f32).reshape(16, 128).T
    vecs[:, 48:56] = np.asarray(pool_scale, f32)[0].reshape(8, 128).T
    common = {"w_in": w_in2, "w_out": w_out2, "w_up": w_up2, "w_down": w_down2, "wsT": wsT, "mask": mask,
              "wpool": wpool, "gvt": gvt, "bst": bst, "vecs": vecs}
    in_maps = []
    wins = np.array([2, 4, 8, 16], f32)
    for c in range(NCORES):
        b, q = divmod(c, 4)
        t0 = q * TOK
        xT = np.ascontiguousarray(x[b, t0:t0 + TOK, :].T)
        if q == 0:
            xh = np.zeros((D, HALO), f32)
            cnt = np.minimum(np.arange(HALO, dtype=f32)[None, :] + 1.0, wins[:, None])
        else:
            xh = np.ascontiguousarray(x[b, t0 - HALO:t0, :].T)
            cnt = np.broadcast_to(wins[:, None], (4, HALO))
        invc = np.ascontiguousarray(np.broadcast_to((1.0 / cnt).astype(f32)[None], (128, 4, HALO)))
        m = dict(common)
        m["xT"] = xT
        m["xh"] = xh
        m["invc"] = invc
        in_maps.append(m)
    return in_maps


def kernel(x, g_mix, w_in, g_v, w_s, b_s, w_pool, pool_scale, w_out, g_ffn, w_up, w_down, g_final):
    in_maps = _host_inputs(x, g_mix, w_in, g_v, w_s, b_s, w_pool, pool_scale, w_out, g_ffn, w_up, w_down, g_final)
    nc = build_program()
    res = run_bass_kernel_spmd(nc, in_maps, core_ids=list(range(NCORES)))
    out = np.empty((2, SEQ, D), np.float32)
    for c in range(NCORES):
        b, q = divmod(c, 4)
        out[b, q * TOK:(q + 1) * TOK, :] = np.asarray(res.results[c]["outT"], np.float32).T
    return out
```

```python
from contextlib import ExitStack

import numpy as np
import concourse.bass as bass
import concourse.mybir as mybir
from concourse.bass_utils import run_bass_kernel_spmd

F32 = mybir.dt.float32
BF16 = mybir.dt.bfloat16
AF = mybir.ActivationFunctionType
ALU = mybir.AluOpType
AX = mybir.AxisListType

NCORES = 8
D = 2048
TOK = 1024
SEQ = 4096
KC = 16
HALO = 16
ZL = TOK + HALO
DFF = 8192
EPS = 1e-6
NSLOT = 5
SLABW = 256
FB = 16
NFB = DFF // (FB * 128)


class Op:
    __slots__ = ("eng", "fn", "deps", "kind", "signal", "semkey", "val")

    def __init__(self, eng, fn, deps, kind, semkey):
        self.eng = eng
        self.fn = fn
        self.deps = [d for d in deps if d is not None]
        self.kind = kind
        self.signal = False
        self.semkey = semkey
        self.val = None


class Prog:
    ENGS = ("pe", "act", "dve", "pool", "sp")

    def __init__(self):
        self.ops = {e: [] for e in self.ENGS}
        self.dma_counts = {}

    def add(self, eng, fn, deps=()):
        op = Op(eng, fn, deps, "c", eng)
        for d in op.deps:
            d.signal = True
        self.ops[eng].append(op)
        return op

    def dma(self, queue, fn, semname, deps=()):
        op = Op(queue, fn, deps, "d", "dma_" + semname)
        for d in op.deps:
            d.signal = True
        cnt = self.dma_counts.get(op.semkey, 0) + 16
        self.dma_counts[op.semkey] = cnt
        op.val = cnt
        self.ops[queue].append(op)
        return op

    def finalize(self):
        for e in self.ENGS:
            c = 0
            for op in self.ops[e]:
                if op.kind == "c" and op.signal:
                    c += 1
                    op.val = c
        return sorted(set(self.ENGS) | set(self.dma_counts.keys()))

    def emit(self, eng, e, sems):
        waited = {}
        for op in self.ops[eng]:
            for d in op.deps:
                if d.kind == "c" and d.eng == eng and eng in ("pe", "sp"):
                    continue
                assert d.val is not None
                if waited.get(d.semkey, 0) >= d.val:
                    continue
                e.wait_ge(sems[d.semkey], d.val)
                waited[d.semkey] = d.val
            ins = op.fn(e)
            if op.kind == "d":
                ins.then_inc(sems[op.semkey], 16)
            elif op.signal:
                ins.then_inc(sems[eng], 1)


def build_program(debug=False):
    nc = bass.Bass("TRN2", target_bir_lowering=False)
    P = Prog()

    def dram_in(name, shape):
        return nc.dram_tensor(name, list(shape), F32, kind="ExternalInput").ap()

    xT_d = dram_in("xT", [D, TOK]).rearrange("(k p) t -> p k t", p=128)
    xh_d = dram_in("xh", [D, HALO]).rearrange("(k p) t -> p k t", p=128)
    w_in_d = dram_in("w_in", [D, 3072]).rearrange("(k p) n -> p k n", p=128)
    w_out_d = dram_in("w_out", [D, D]).rearrange("(k p) n -> p k n", p=128)
    w_up_d = dram_in("w_up", [D, DFF]).rearrange("(k p) n -> p k n", p=128)
    w_down_d = dram_in("w_down", [DFF, D]).rearrange("(k p) n -> p k n", p=128)
    wsT_d = dram_in("wsT", [128, 8, 128])
    mask_d = dram_in("mask", [128, 8, 128])
    wpool_d = dram_in("wpool", [128, 4, 2, 256])
    gv_d = dram_in("gvt", [128, 1024])
    bs_d = dram_in("bst", [128, 1024])
    vecs_d = dram_in("vecs", [128, 56])
    invc_d = dram_in("invc", [128, 4, 16])
    out_d = nc.dram_tensor("outT", [D, TOK], F32, kind="ExternalOutput").ap()

    B0 = 16512
    X0 = B0
    H0 = X0 + 66048
    S0 = H0 + 32768
    R0 = S0 + 33024
    C0 = R0 + NSLOT * 8192
    cur = [C0]

    def sb(name, shape, dt, off):
        return nc.alloc_sbuf_tensor_at(name, list(shape), dt, offset=off)

    def calloc(name, shape, dt, nbytes):
        off = cur[0]
        cur[0] += (nbytes + 31) // 32 * 32
        assert cur[0] <= 229376, cur[0]
        return sb(name, shape, dt, off)

    xf = sb("xf", [128, KC, TOK], F32, X0)
    uT = sb("uT", [128, 8, TOK], BF16, X0)
    zE = sb("zE", [128, 8, ZL], F32, X0 + 16384)
    pooled = sb("pooled", [128, 8, TOK], BF16, X0 + 49664)
    xr_mid = sb("xr_mid", [128, 8, TOK], F32, X0 + 16384)
    xr_h = sb("xr_h", [128, 8, TOK], F32, H0)
    hT = sb("hT", [128, KC, TOK], BF16, H0)
    vh = sb("vh", [128, 8, TOK], BF16, S0)
    vf = sb("vf", [128, 2, TOK], F32, S0 + 16384)
    sqv = sb("sqv", [128, 1024], F32, S0 + 24576)
    mtmp = sb("mtmp", [128, 2, 512], F32, S0 + 28672)
    h2T = sb("h2T", [128, KC, TOK], BF16, S0)
    rsr = sb("rsr", [1, TOK], F32, S0 + 16384)
    rs_hi = sb("rs_hi", [1, TOK], BF16, S0 + 20480)
    rs_lo = sb("rs_lo", [1, TOK], BF16, S0 + 22528)
    rs_lo2 = sb("rs_lo2", [1, TOK], BF16, S0 + 24576)
    ringall = sb("ringall", [128, NSLOT, KC, SLABW], BF16, R0)

    gv_t = calloc("gv_t", [128, 1024], F32, 4096)
    rtmp = sb("rtmp", [128, 2, 512], F32, C0)
    bs_t = calloc("bs_t", [128, 1024], F32, 4096)
    rstd_bc = calloc("rstd_bc", [128, TOK], F32, 4096)
    wpool = calloc("wpool", [128, 4, 2, 256], BF16, 4096)
    wsT = calloc("wsT", [128, 8, 128], BF16, 2048)
    maskt = calloc("maskt", [128, 8, 128], BF16, 2048)
    sqr = calloc("sqr", [128, 2, TOK], BF16, 4096)
    xst_off = cur[0]
    xstage = [calloc("xstage%d" % i, [128, TOK], F32, 4160) for i in range(2)]
    tmpA = sb("tmpA", [128, ZL], F32, xst_off)
    tmpB = sb("tmpB", [128, ZL], F32, xst_off + 4160)
    ones = calloc("ones", [128, 128], BF16, 256)
    vecs = calloc("vecs", [128, 56], F32, 224)
    invc = calloc("invc", [128, 4, 16], F32, 256)
    xh = calloc("xh", [128, KC, HALO], F32, 1024)
    hh = calloc("hh", [128, KC, HALO], BF16, 512)
    sqh = calloc("sqh", [128, KC, HALO], BF16, 512)
    rstd_h = calloc("rstd_h", [128, HALO], F32, 64)
    vstat = calloc("vstat", [128, 2, 8], F32, 64)
    vrs = calloc("vrs", [128, 2, 8], F32, 64)
    ptmp = calloc("ptmp", [128, 16], F32, 64)
    epst = calloc("epst", [128, 1], F32, 32)
    onef = calloc("onef", [128, 1], F32, 32)
    rstd_col = calloc("rstd_col", [128, 8], F32, 32)

    G_MIX, G_FFN, G_FIN, PSC = 0, 16, 32, 48

    def MIX(k):
        return uT[:, k, :] if k < 8 else pooled[:, k - 8, :]

    def XR(m):
        if 4 <= m < 12:
            return xr_mid[:, m - 4, :]
        return xr_h[:, m if m < 4 else m - 8, :]

    def RS(n, k, a, b):
        return ringall[:, n % NSLOT, k, a:b]

    ps = nc.alloc_psum_tensor("ps", [128, 8, 512], F32)
    NRING = 6
    bank_last = [None] * 8
    bank_next = [0]

    def take_bank():
        b = bank_next[0]
        bank_next[0] = (b + 1) % NRING
        return b

    def pe_job(segments, nb=2):
        banks = [take_bank() for _ in range(nb)]
        first = True
        last = None
        for seg in segments:
            if callable(seg):
                seg()
                continue
            mms, deps = seg
            d = list(deps() if callable(deps) else deps)
            if first:
                d += [bank_last[b] for b in banks]
                first = False

            def fn(e, mms=mms):
                ins = None
                for (sel, c0, c1, lhsT, rhs, st, sp_) in mms:
                    ins = e.matmul(ps[:, banks[sel], c0:c1], lhsT, rhs, start=st, stop=sp_)
                return ins
            last = P.add("pe", fn, d)
        return last, banks

    def proj_mms(lhs_fn, rhs_fn, ks, k_first, k_last):
        mms = []
        for k in ks:
            mms.append((0, 0, 512, lhs_fn(k), rhs_fn(k)[:, 0:512], k == k_first, k == k_last))
            mms.append((1, 0, 512, lhs_fn(k), rhs_fn(k)[:, 512:1024], k == k_first, k == k_last))
        return mms

    vecs_ld = P.dma("sp", lambda e: e.dma_start(out=vecs[:, :], in_=vecs_d[:, :]), "cst0")
    slab0a = [P.dma("pool", lambda e: e.dma_start(out=ringall[:, 0, :, 0:128], in_=w_in_d[:, :, 2048:2048 + 128]), "slot0a")]
    x_ld = []
    for q in range(4):
        x_ld.append(P.dma("sp", lambda e, q=q: e.dma_start(out=xf[:, 4 * q:4 * q + 4, :], in_=xT_d[:, 4 * q:4 * q + 4, :]),
                          "x%d" % q, [slab0a[0]]))
    cst1 = None
    for fn in (lambda e: e.dma_start(out=xh[:, :, :], in_=xh_d[:, :, :]),
               lambda e: e.dma_start(out=invc[:, :, :], in_=invc_d[:, :, :])):
        cst1 = P.dma("sp", fn, "cst1")

    slabs = []
    for j in range(4):
        slabs.append(w_in_d[:, :, 2048 + SLABW * j:2048 + SLABW * (j + 1)])
    for j in range(4):
        slabs.append(w_in_d[:, :, SLABW * j:SLABW * (j + 1)])
    for j in range(4):
        slabs.append(w_in_d[:, :, 1024 + SLABW * j:1024 + SLABW * (j + 1)])
    for _rep in range(2):
        for j in range(8):
            slabs.append(w_out_d[:, :, SLABW * j:SLABW * (j + 1)])
    for fb in range(NFB):
        for j in range(8):
            c0 = fb * FB * 128 + SLABW * j
            slabs.append(w_up_d[:, :, c0:c0 + SLABW])
        for j in range(8):
            slabs.append(w_down_d[:, fb * FB:(fb + 1) * FB, SLABW * j:SLABW * (j + 1)])
    for j in range(8):
        slabs.append(w_down_d[:, (NFB - 1) * FB:NFB * FB, SLABW * j:SLABW * (j + 1)])
    NSLAB = len(slabs)
    slab_ld = [None] * NSLAB
    slab_done = [None] * NSLAB
    nxt = [0]

    def issue_loads(upto):
        upto = min(upto, NSLAB - 1)
        while nxt[0] <= upto:
            n = nxt[0]
            deps = []
            if n >= NSLOT:
                assert slab_done[n - NSLOT] is not None, n
                deps.append(slab_done[n - NSLOT])
            else:
                deps.append(x_ld[3])
            if n == 0:
                slab_ld[0] = P.dma("pool", lambda e: e.dma_start(out=ringall[:, 0, :, 128:256], in_=slabs[0][:, :, 128:256]), "slot0", deps)
            else:
                slab_ld[n] = P.dma("pool", lambda e, n=n: e.dma_start(out=ringall[:, n % NSLOT, :, :], in_=slabs[n]),
                                   "slot%d" % (n % NSLOT), deps)
            nxt[0] += 1

    issue_loads(NSLOT - 1)
    cstp = None
    for fn in (lambda e: e.dma_start(out=wsT[:, :, :], in_=wsT_d[:, :, :]),
               lambda e: e.dma_start(out=maskt[:, :, :], in_=mask_d[:, :, :]),
               lambda e: e.dma_start(out=wpool[:, :, :, :], in_=wpool_d[:, :, :, :])):
        cstp = P.dma("pool", fn, "cstp", [x_ld[3]])
    cst = None
    for fn in (lambda e: e.dma_start(out=gv_t[:, :], in_=gv_d[:, :]),
               lambda e: e.dma_start(out=bs_t[:, :], in_=bs_d[:, :])):
        cst = P.dma("sp", fn, "cst", [slab_ld[NSLOT - 1]])

    ones_op = P.add("dve", lambda e: e.memset(ones[:, :], 1.0))
    eps_op = P.add("dve", lambda e: e.memset(epst[:, :], EPS))
    onef_op = P.add("dve", lambda e: e.memset(onef[:, :], 1.0))
    EPSB = epst[:, 0:1]

    sq_reader = [None, None]

    def stat_chunk(k, src_ap, src_deps, eng):
        s = k % 2
        if eng == "act":
            sq = P.add("act", lambda e: e.activation(out=sqr[:, s, :], in_=src_ap, func=AF.Square), list(src_deps) + [sq_reader[s]])
        else:
            sq = P.add("dve", lambda e: e.tensor_tensor(out=sqr[:, s, :], in0=src_ap, in1=src_ap, op=ALU.mult), list(src_deps) + [sq_reader[s]])

        def mk():
            def fn(e):
                e.matmul(ps[:, 6, :], ones[:, :], sqr[:, s, 0:512], start=(k == 0), stop=(k == KC - 1))
                return e.matmul(ps[:, 7, :], ones[:, :], sqr[:, s, 512:1024], start=(k == 0), stop=(k == KC - 1))
            d = [sq, ones_op]
            if k == 0:
                d += [bank_last[6], bank_last[7]]
            pe = P.add("pe", fn, d)
            sq_reader[s] = pe
            return pe
        return sq, mk

    def rstd_chain(pe_op, war_deps):
        s1 = P.add("act", lambda e: e.activation(out=rstd_bc[:, 0:512], in_=ps[:, 6, :], func=AF.Ln, bias=EPSB, scale=1.0 / D),
                   [pe_op, eps_op] + list(war_deps))
        s2 = P.add("act", lambda e: e.activation(out=rstd_bc[:, 512:1024], in_=ps[:, 7, :], func=AF.Ln, bias=EPSB, scale=1.0 / D),
                   [pe_op, eps_op] + list(war_deps))
        bank_last[6] = s1
        bank_last[7] = s2
        P.add("act", lambda e: e.activation(out=rstd_bc[:, 0:512], in_=rstd_bc[:, 0:512], func=AF.Exp, scale=-0.5), [s1])
        return P.add("act", lambda e: e.activation(out=rstd_bc[:, 512:1024], in_=rstd_bc[:, 512:1024], func=AF.Exp, scale=-0.5), [s2])

    h_ops = [None] * KC
    n1_mm = [None]
    z_evac = [None] * 8
    zh_last = [None]
    last_rstd_reader = [None]

    def z_evacs(m, job, banks, hjob):
        e1 = P.add("dve", lambda e: e.tensor_tensor(out=zE[:, m, HALO:HALO + 512], in0=ps[:, banks[0], :], in1=rstd_bc[:, 0:512], op=ALU.mult), [job, r1[0]])
        e2 = P.add("dve", lambda e: e.tensor_tensor(out=zE[:, m, HALO + 512:ZL], in0=ps[:, banks[1], :], in1=rstd_bc[:, 512:1024], op=ALU.mult), [job, r1[0]])
        e3 = P.add("dve", lambda e: e.tensor_tensor(out=zE[:, m, 0:HALO], in0=ps[:, 7, m * HALO:(m + 1) * HALO], in1=rstd_h[:, :], op=ALU.mult), [hjob, hs2])
        bank_last[banks[0]] = e1
        bank_last[banks[1]] = e2
        bank_last[7] = e3
        z_evac[m] = (e1, e2, e3)
        last_rstd_reader[0] = e2

    def z_halo_job(m, n, mmi):
        def halo_fn(e):
            ins = None
            for k in range(KC):
                ins = e.matmul(ps[:, 7, m * HALO:(m + 1) * HALO], RS(n, k, mmi * 128, (mmi + 1) * 128), hh[:, k, :],
                               start=(k == 0), stop=(k == KC - 1))
            return ins
        return P.add("pe", halo_fn, [slab_ld[n] if (n, mmi) != (0, 0) else slab0a[0], hh_all, bank_last[7]])

    segs = []
    for k in range(KC):
        def hook(k=k):
            xk = xf[:, k, :]
            ld = x_ld[k // 4]
            if k % 2 == 0:
                h_ops[k] = P.add("act", lambda e: e.activation(out=hT[:, k, :], in_=xk, func=AF.Copy, scale=vecs[:, G_MIX + k:G_MIX + k + 1]), [ld, vecs_ld])
                sq, mk = stat_chunk(k, xk, [ld], "dve")
            else:
                h_ops[k] = P.add("dve", lambda e: e.tensor_scalar(out=hT[:, k, :], in0=xk, scalar1=vecs[:, G_MIX + k:G_MIX + k + 1], scalar2=None, op0=ALU.mult), [ld, vecs_ld])
                sq, mk = stat_chunk(k, xk, [ld], "act")
            n1_mm[0] = mk()
        segs.append(hook)
        segs.append((proj_mms(lambda kk: RS(0, kk, 0, 128), lambda kk: hT[:, kk, :], [k], 0, KC - 1), lambda k=k: [slab0a[0], h_ops[k]]))
    zjob0, zbanks0 = pe_job(segs)
    r1 = [rstd_chain(n1_mm[0], [])]

    sqh_op = P.add("dve", lambda e: e.tensor_tensor(out=sqh[:, :, :], in0=xh[:, :, :], in1=xh[:, :, :], op=ALU.mult), [cst1])
    bh = take_bank()

    def halo_stat(e):
        ins = None
        for k in range(KC):
            ins = e.matmul(ps[:, bh, 0:HALO], ones[:, :], sqh[:, k, :], start=(k == 0), stop=(k == KC - 1))
        return ins
    hs_pe = P.add("pe", halo_stat, [sqh_op, ones_op, bank_last[bh]])
    hs1 = P.add("act", lambda e: e.activation(out=rstd_h[:, :], in_=ps[:, bh, 0:HALO], func=AF.Ln, bias=EPSB, scale=1.0 / D), [hs_pe, eps_op])
    bank_last[bh] = hs1
    hs2 = P.add("act", lambda e: e.activation(out=rstd_h[:, :], in_=rstd_h[:, :], func=AF.Exp, scale=-0.5), [hs1])

    h_all = [h_ops[KC - 1], h_ops[KC - 2]]

    hh_ops = []
    for k in range(KC):
        hh_ops.append(P.add("dve", lambda e, k=k: e.tensor_scalar(out=hh[:, k, :], in0=xh[:, k, :], scalar1=vecs[:, G_MIX + k:G_MIX + k + 1],
                                                                   scalar2=None, op0=ALU.mult), [cst1, vecs_ld]))
    hh_all = hh_ops[-1]

    pool_ops = [None] * 8
    prev_tmp_user = [None]

    def pooling_chunk(c):
        g = c // 2
        win = 2 << g
        E = zE[:, c, :]
        deps0 = list(z_evac[c]) + [prev_tmp_user[0]]
        o = P.add("dve", lambda e: e.tensor_tensor(out=tmpA[:, 1:ZL], in0=E[:, 1:ZL], in1=E[:, 0:ZL - 1], op=ALU.add), deps0)
        src, dst = tmpA, tmpB
        sh, lo = 2, 1
        while sh < win:
            lo2 = lo + sh
            o = P.add("dve", lambda e, src=src, dst=dst, sh=sh, lo2=lo2: e.tensor_tensor(
                out=dst[:, lo2:ZL], in0=src[:, lo2:ZL], in1=src[:, lo2 - sh:ZL - sh], op=ALU.add), [o])
            src, dst = dst, src
            lo = lo2
            sh *= 2
        assert lo <= HALO
        fin = P.add("dve", lambda e, src=src: e.scalar_tensor_tensor(
            out=pooled[:, c, :], in0=src[:, HALO:ZL], scalar=1.0 / win, in1=E[:, HALO:ZL], op0=ALU.mult, op1=ALU.subtract), [o])
        f1 = P.add("dve", lambda e, src=src: e.tensor_tensor(out=ptmp[:, :], in0=src[:, HALO:2 * HALO], in1=invc[:, g, :], op=ALU.mult), [fin, cst1])
        f2 = P.add("dve", lambda e: e.tensor_tensor(out=pooled[:, c, 0:HALO], in0=ptmp[:, :], in1=E[:, HALO:2 * HALO], op=ALU.subtract), [f1])
        prev_tmp_user[0] = f2
        pool_ops[c] = f2

    c1 = P.add("dve", lambda e: e.tensor_copy(out=rs_hi[:, :], in_=rstd_bc[0:1, :]), [r1[0]])
    c2 = P.add("dve", lambda e: e.tensor_tensor(out=rsr[:, :], in0=rstd_bc[0:1, :], in1=rs_hi[:, :], op=ALU.subtract), [c1])
    c3 = P.add("dve", lambda e: e.tensor_copy(out=rs_lo[:, :], in_=rsr[:, :]), [c2])
    c4 = P.add("dve", lambda e: e.tensor_tensor(out=rsr[:, :], in0=rsr[:, :], in1=rs_lo[:, :], op=ALU.subtract), [c3])
    c5 = P.add("dve", lambda e: e.tensor_copy(out=rs_lo2[:, :], in_=rsr[:, :]), [c4])
    hj = z_halo_job(0, 0, 0)
    z_evacs(0, zjob0, zbanks0, hj)
    rc_evac = None
    for m in range(1, 8):
        n, mmi = m // 2, m % 2
        if mmi == 0:
            issue_loads(n + NSLOT - 1)
        job, banks = pe_job([(proj_mms(lambda k: RS(n, k, mmi * 128, (mmi + 1) * 128), lambda k: hT[:, k, :], range(KC), 0, KC - 1),
                              [slab_ld[n]] + h_all)])
        hj = z_halo_job(m, n, mmi)
        slab_done[n] = hj
        z_evacs(m, job, banks, hj)
        if m % 2 == 1:
            pooling_chunk(m // 2)
        if m == 1:
            slab_done[0] = hj
        if m == 3:
            bc = take_bank()

            def col_fn(e, bc=bc):
                ins = None
                first = True
                for i in range(8):
                    for t in (rs_hi, rs_lo, rs_lo2):
                        ins = e.matmul(ps[:, bc, i:i + 1], t[0:1, i * 128:(i + 1) * 128], ones[0:1, 0:1], start=first, stop=(i == 7 and t is rs_lo2))
                        first = False
                return ins
            col_pe = P.add("pe", col_fn, [c1, c3, c5, ones_op, bank_last[bc]])
            rc_evac = P.add("act", lambda e, bc=bc: e.activation(out=rstd_col[:, :], in_=ps[:, bc, 0:8], func=AF.Copy), [col_pe])
            bank_last[bc] = rc_evac

    ut_reader = [None, None]
    ut_ctr = [0]
    u_evac_last = None
    for m in range(8):
        n, mmi = 4 + m // 2, m % 2
        if mmi == 0:
            issue_loads(n + NSLOT - 1)
        job, banks = pe_job([(proj_mms(lambda k: RS(n, k, mmi * 128, (mmi + 1) * 128), lambda k: hT[:, k, :], range(KC), 0, KC - 1),
                              [slab_ld[n]] + h_all)])
        slab_done[n] = job
        for half in range(2):
            s = ut_ctr[0] % 2
            ut_ctr[0] += 1
            a1 = P.add("dve", lambda e, s=s, half=half, b=banks[half]: e.tensor_tensor(
                out=mtmp[:, s, :], in0=ps[:, b, :], in1=rstd_bc[:, half * 512:(half + 1) * 512], op=ALU.mult), [job, r1[0], ut_reader[s]])
            bank_last[banks[half]] = a1
            a2 = P.add("act", lambda e, s=s, half=half, m=m: e.activation(out=uT[:, m, half * 512:(half + 1) * 512], in_=mtmp[:, s, :], func=AF.Gelu_apprx_tanh), [a1])
            ut_reader[s] = a2
            u_evac_last = a2
            last_rstd_reader[0] = a1
        if m % 2 == 0:
            pooling_chunk(4 + m // 2)

    pool_all = pool_ops[7]

    ws_op = P.add("dve", lambda e: e.tensor_tensor(out=wsT[:, :, :], in0=wsT[:, :, :], in1=maskt[:, :, :], op=ALU.mult), [cstp])
    issue_loads(12)
    v_deps = [slab_ld[8 + j] for j in range(4)]
    vf_readers = [None, None]
    vst_readers = [None, None]
    vh_ops = [None] * 8
    v_jobs = []
    mix_last = [None]
    mix_tile_last = [None] * 8
    mt_readers = [ut_reader[0], ut_reader[1]]
    mt_ctr = [0]

    def emit_mixing(i):
        for hb in range(2):
            ba = take_bank()

            def fn(e, ba=ba, hb=hb):
                ins = None
                for hq in range(4):
                    h = 4 * hb + hq
                    ins = e.matmul(ps[:, ba, hq * 128:(hq + 1) * 128], vh[:, i, h * 128:(h + 1) * 128], wsT[:, h, :], start=True, stop=True)
                return ins
            job = P.add("pe", fn, [vh_ops[i], ws_op, bank_last[ba]])
            s = mt_ctr[0] % 2
            mt_ctr[0] += 1
            a1 = P.add("dve", lambda e, s=s, ba=ba, hb=hb: e.tensor_tensor(
                out=mtmp[:, s, :], in0=ps[:, ba, :], in1=bs_t[:, 512 * hb:512 * hb + 512], op=ALU.add), [job, cst, mt_readers[s]])
            bank_last[ba] = a1
            a2 = P.add("dve", lambda e, s=s, hb=hb: e.tensor_tensor(
                out=uT[:, 4 * hb:4 * hb + 4, i * 128:(i + 1) * 128],
                in0=mtmp[:, s, :].rearrange("p (h t) -> p h t", h=4),
                in1=uT[:, 4 * hb:4 * hb + 4, i * 128:(i + 1) * 128], op=ALU.mult), [a1, u_evac_last])
            mt_readers[s] = a2
            mix_last[0] = a2
            mix_tile_last[i] = a2

    ob_last = [None]

    def emit_pool_mm(g):
        if True:
            jobs = []
            for mmi in range(2):
                mms = []
                for kc in range(2):
                    lhsT = wpool[:, g, kc, mmi * 128:(mmi + 1) * 128]
                    mms.append((0, 0, 512, lhsT, pooled[:, 2 * g + kc, 0:512], kc == 0, kc == 1))
                    mms.append((1, 0, 512, lhsT, pooled[:, 2 * g + kc, 512:1024], kc == 0, kc == 1))
                jobs.append(pe_job([(mms, [cstp, pool_ops[2 * g], pool_ops[2 * g + 1]])]))
            for mmi in range(2):
                job, banks = jobs[mmi]
                c = 2 * g + mmi
                for half in range(2):
                    ev = P.add("act", lambda e, c=c, half=half, b=banks[half]: e.activation(
                        out=pooled[:, c, half * 512:(half + 1) * 512], in_=ps[:, b, :], func=AF.Copy, scale=vecs[:, PSC + c:PSC + c + 1]),
                        [jobs[0][0], jobs[1][0], cst])
                    bank_last[banks[half]] = ev
                    ob_last[0] = ev

    for i in range(8):
        mms = []
        for k in range(KC):
            lhsT = hT[:, k, i * 128:(i + 1) * 128]
            mms.append((0, 0, 512, lhsT, ringall[:, 3:5, k, :], k == 0, k == KC - 1))
            mms.append((1, 0, 512, lhsT, ringall[:, 0:2, k, :], k == 0, k == KC - 1))
        job, banks = pe_job([(mms, v_deps + h_all)])
        v_jobs.append(job)
        s = i % 2
        g1 = P.add("act", lambda e, s=s, i=i, b=banks[0]: e.activation(out=vf[:, s, 0:512], in_=ps[:, b, :], func=AF.Gelu_apprx_tanh,
                                                                        scale=rstd_col[:, i:i + 1]), [job, rc_evac, vf_readers[s]])
        g2 = P.add("act", lambda e, s=s, i=i, b=banks[1]: e.activation(out=vf[:, s, 512:1024], in_=ps[:, b, :], func=AF.Gelu_apprx_tanh,
                                                                        scale=rstd_col[:, i:i + 1]), [job, rc_evac, vf_readers[s]])
        bank_last[banks[0]] = g1
        bank_last[banks[1]] = g2
        if 3 <= i <= 6:
            emit_pool_mm(i - 3)
        q2 = None
        for h in range(8):
            q2 = P.add("act", lambda e, s=s, h=h: e.activation(out=sqv[:, h * 128:(h + 1) * 128], in_=vf[:, s, h * 128:(h + 1) * 128], func=AF.Square,
                                                              accum_out=vstat[:, s, h:h + 1]), [g1, g2, vst_readers[s]])
        q3 = P.add("act", lambda e, s=s: e.activation(out=vrs[:, s, :], in_=vstat[:, s, :], func=AF.Ln, bias=EPSB, scale=1.0 / 128), [q2, eps_op])
        q4 = P.add("act", lambda e, s=s: e.activation(out=vrs[:, s, :], in_=vrs[:, s, :], func=AF.Exp, scale=-0.5), [q3])
        if i >= 2:
            emit_mixing(i - 2)
        if i == 7:
            emit_mixing(6)
        last = None
        for h in range(8):
            last = P.add("dve", lambda e, s=s, h=h, i=i: e.scalar_tensor_tensor(
                out=vh[:, i, h * 128:(h + 1) * 128], in0=vf[:, s, h * 128:(h + 1) * 128], scalar=vrs[:, s, h:h + 1],
                in1=gv_t[:, h * 128:(h + 1) * 128], op0=ALU.mult, op1=ALU.mult), [q4, cst])
        vf_readers[s] = last
        vst_readers[s] = last
        vh_ops[i] = last
    for j in range(4):
        slab_done[8 + j] = v_jobs[-1]
    h_dead = v_jobs[-1]

    xs4 = [xstage[0][:, 0:512], xstage[0][:, 512:1024], xstage[1][:, 0:512], xstage[1][:, 512:1024]]
    xs_reader4 = [pool_all] * 4
    order = [(half, m) for half in range(2) for m in range(KC)]
    xs_ld = {}

    def load_xs(idx):
        half, m = order[idx]
        s = idx % 4
        xs_ld[idx] = P.dma("sp", lambda e: e.dma_start(out=xs4[s], in_=xT_d[:, m, half * 512:(half + 1) * 512]), "xs%d" % s, [xs_reader4[s]])

    for idx in range(4):
        load_xs(idx)
    x1_e = [[None] * KC, [None] * KC]
    h2h = [[None] * KC, [None] * KC]
    hs_reader = [sq_reader[0], sq_reader[0], sq_reader[1], sq_reader[1]]
    hs_ctr = [0]
    pend = []
    rexp2 = [None, None]
    h2_defer = []
    wo_last = None
    korder = list(range(8, 16)) + list(range(8))

    def flush_one():
        pe, half, m = pend.pop(0)()
        if m == KC - 1:
            c0, c1 = half * 512, (half + 1) * 512
            ln = P.add("act", lambda e: e.activation(out=rstd_bc[:, c0:c1], in_=ps[:, 6 + half, :], func=AF.Ln, bias=EPSB, scale=1.0 / D),
                       [pe, eps_op, last_rstd_reader[0]])
            bank_last[6 + half] = ln
            rexp2[half] = P.add("act", lambda e: e.activation(out=rstd_bc[:, c0:c1], in_=rstd_bc[:, c0:c1], func=AF.Exp, scale=-0.5), [ln])

    for idx, (half, m) in enumerate(order):
        c0, c1 = half * 512, (half + 1) * 512
        n, mmi = 12 + half * 8 + m // 2, m % 2
        if mmi == 0:
            issue_loads(n + NSLOT - 1)
        mms = [(0, 0, 512, RS(n, k, mmi * 128, (mmi + 1) * 128), MIX(k)[:, c0:c1], k == korder[0], k == korder[-1]) for k in korder]
        job, banks = pe_job([(mms, [slab_ld[n], ob_last[0], mix_tile_last[3 if half == 0 else 7]])], nb=1)
        slab_done[n] = job
        wo_last = job
        if idx == 2:
            emit_mixing(7)
            for f_ in h2_defer:
                f_()
            h2_defer = None
        while len(pend) > (3 if half == 0 else 1):
            flush_one()
        ev = P.add("dve", lambda e, m=m, b=banks[0], c0=c0, c1=c1, xs=xs4[idx % 4]: e.tensor_tensor(out=XR(m)[:, c0:c1], in0=ps[:, b, :], in1=xs, op=ALU.add),
                   [job, xs_ld[idx], h_dead, pool_all])
        bank_last[banks[0]] = ev
        x1_e[half][m] = ev
        xs_reader4[idx % 4] = ev
        if idx + 4 < len(order):
            load_xs(idx + 4)
        hsl = hs_ctr[0] % 4
        hs_ctr[0] += 1
        sq_ap = sqr[:, hsl // 2, (hsl % 2) * 512:(hsl % 2) * 512 + 512]
        sq = P.add("act", lambda e, m=m, sq_ap=sq_ap, c0=c0, c1=c1: e.activation(out=sq_ap, in_=XR(m)[:, c0:c1], func=AF.Square), [ev, hs_reader[hsl]])

        def mk(m=m, half=half, sq=sq, sq_ap=sq_ap, hsl=hsl):
            d = [sq, ones_op]
            if m == 0:
                d.append(bank_last[6 + half])
            pe = P.add("pe", lambda e: e.matmul(ps[:, 6 + half, :], ones[:, :], sq_ap, start=(m == 0), stop=(m == KC - 1)), d)
            hs_reader[hsl] = pe
            return pe, half, m
        pend.append(mk)

        def mk_h2(m=m, half=half, ev=ev, c0=c0, c1=c1):
            h2h[half][m] = P.add("act", lambda e: e.activation(out=h2T[:, m, c0:c1], in_=XR(m)[:, c0:c1], func=AF.Copy,
                                                              scale=vecs[:, G_FFN + m:G_FFN + m + 1]), [ev, mix_tile_last[7], cst])
        if h2_defer is not None:
            h2_defer.append(mk_h2)
        else:
            mk_h2()
    x1_ops = x1_e[1]
    h2_ops = h2h[1]

    rt_readers = [None, None]
    rt_ctr = [0]
    x2_ops = list(x1_ops)
    x2_e1 = list(x1_e[0])
    nbase = 28
    r2 = [None]
    prev_down_last = None
    fin_mk = [None] * KC
    n3_mm = [None]
    for fb in range(NFB):
        act_ops = [None] * FB
        lastblk = fb == NFB - 1
        for f in range(FB):
            n, mmi = nbase + fb * 16 + f // 2, f % 2
            if mmi == 0:
                issue_loads(n + NSLOT - 1)
            lhs = lambda k: RS(n, k, mmi * 128, (mmi + 1) * 128)
            rhs = lambda k: h2T[:, k, :]
            if fb == 0 and f == 0:
                segs = []
                for k in range(KC):
                    if k == KC - 1:
                        def hook():
                            while pend:
                                flush_one()
                        segs.append(hook)
                    segs.append((proj_mms(lhs, rhs, [k], 0, KC - 1), [slab_ld[n], h2_ops[k], wo_last]))
                job, banks = pe_job(segs)
                r2[0] = rexp2[1]
            else:
                job, banks = pe_job([(proj_mms(lhs, rhs, range(KC), 0, KC - 1), [slab_ld[n], h2_ops[KC - 1]])])
            slab_done[n] = job
            last = None
            for half in range(2):
                s = rt_ctr[0] % 2
                rt_ctr[0] += 1
                a1 = P.add("dve", lambda e, s=s, half=half, b=banks[half]: e.scalar_tensor_tensor(
                    out=rtmp[:, s, :], in0=ps[:, b, :], scalar=0.0, in1=rstd_bc[:, half * 512:(half + 1) * 512], op0=ALU.max, op1=ALU.mult),
                    [job, r2[0], rt_readers[s], vh_ops[7]])
                bank_last[banks[half]] = a1
                last_rstd_reader[0] = a1
                a2 = P.add("act", lambda e, s=s, f=f, half=half: e.activation(out=MIX(f)[:, half * 512:(half + 1) * 512], in_=rtmp[:, s, :], func=AF.Square),
                           [a1, wo_last, prev_down_last])
                rt_readers[s] = a2
                last = a2
            act_ops[f] = last
        if lastblk:
            break
        dj = None
        for m in range(KC):
            n, mmi = nbase + fb * 16 + 8 + m // 2, m % 2
            if mmi == 0:
                issue_loads(n + NSLOT - 1)
            lhs = lambda k: RS(n, k, mmi * 128, (mmi + 1) * 128)
            if m == 0:
                segs = [(proj_mms(lhs, MIX, [k], 0, FB - 1), [slab_ld[n], act_ops[k]]) for k in range(FB)]
            else:
                segs = [(proj_mms(lhs, MIX, range(FB), 0, FB - 1), [slab_ld[n], act_ops[FB - 1]])]
            job, banks = pe_job(segs)
            slab_done[n] = job
            dj = job
            e1 = P.add("dve", lambda e, m=m, b=banks[0]: e.tensor_tensor(out=XR(m)[:, 0:512], in0=XR(m)[:, 0:512], in1=ps[:, b, :], op=ALU.add),
                       [job, x2_ops[m], x2_e1[m], h2_ops[m]])
            e2 = P.add("dve", lambda e, m=m, b=banks[1]: e.tensor_tensor(out=XR(m)[:, 512:1024], in0=XR(m)[:, 512:1024], in1=ps[:, b, :], op=ALU.add),
                       [job, x2_ops[m], x2_e1[m], h2_ops[m]])
            bank_last[banks[0]] = e1
            bank_last[banks[1]] = e2
            x2_e1[m] = e1
            x2_ops[m] = e2
        prev_down_last = dj

    fb = NFB - 1
    pre_ops = [[None] * KC, [None] * KC]
    rexp = [None, None]

    def final_half_out(k, half):
        c0, c1 = half * 512, (half + 1) * 512
        o = P.add("dve", lambda e: e.tensor_tensor(out=XR(k)[:, c0:c1], in0=XR(k)[:, c0:c1], in1=rstd_bc[:, c0:c1], op=ALU.mult),
                  [rexp[half], pre_ops[half][k]])
        P.dma("sp", lambda e: e.dma_start(out=out_d[k * 128:(k + 1) * 128, c0:c1], in_=XR(k)[:, c0:c1]), "st", [o])

    for half in range(2):
        c0, c1 = half * 512, (half + 1) * 512
        pending = [None]
        last_mm = None
        for m in range(KC):
            n, mmi = nbase + fb * 16 + 8 + half * 8 + m // 2, m % 2
            if mmi == 0:
                issue_loads(n + NSLOT - 1)
            if half == 0 and m == 0:
                segs = [([(0, 0, 512, RS(n, k, mmi * 128, (mmi + 1) * 128), MIX(k)[:, c0:c1], k == 0, k == FB - 1)], [slab_ld[n], act_ops[k]])
                        for k in range(FB)]
            else:
                segs = [([(0, 0, 512, RS(n, k, mmi * 128, (mmi + 1) * 128), MIX(k)[:, c0:c1], k == 0, k == FB - 1) for k in range(FB)],
                         [slab_ld[n], act_ops[FB - 1]])]
            job, banks = pe_job(segs, nb=1)
            slab_done[n] = job
            if pending[0] is not None:
                last_mm = pending[0]()
            ev = P.add("dve", lambda e, m=m, b=banks[0], c0=c0, c1=c1: e.tensor_tensor(out=XR(m)[:, c0:c1], in0=XR(m)[:, c0:c1], in1=ps[:, b, :], op=ALU.add),
                       [job, x2_ops[m], x2_e1[m], h2_ops[m]])
            bank_last[banks[0]] = ev
            hsl = hs_ctr[0] % 4
            hs_ctr[0] += 1
            sq_ap = sqr[:, hsl // 2, (hsl % 2) * 512:(hsl % 2) * 512 + 512]
            sq = P.add("act", lambda e, m=m, sq_ap=sq_ap, c0=c0, c1=c1: e.activation(out=sq_ap, in_=XR(m)[:, c0:c1], func=AF.Square), [ev, hs_reader[hsl]])

            def mk(m=m, sq=sq, sq_ap=sq_ap, hsl=hsl, half=half):
                d = [sq, ones_op]
                if m == 0:
                    d.append(bank_last[6 + half])
                pe = P.add("pe", lambda e: e.matmul(ps[:, 6 + half, :], ones[:, :], sq_ap, start=(m == 0), stop=(m == KC - 1)), d)
                hs_reader[hsl] = pe
                return pe
            pending[0] = mk
            pre_ops[half][m] = P.add("act", lambda e, m=m, c0=c0, c1=c1: e.activation(out=XR(m)[:, c0:c1], in_=XR(m)[:, c0:c1], func=AF.Copy,
                                                                                      scale=vecs[:, G_FIN + m:G_FIN + m + 1]), [ev, sq, cst])
            if half == 1:
                final_half_out(m, 0)
        last_mm = pending[0]()
        ln = P.add("act", lambda e, half=half, c0=c0, c1=c1: e.activation(out=rstd_bc[:, c0:c1], in_=ps[:, 6 + half, :], func=AF.Ln, bias=EPSB, scale=1.0 / D),
                   [last_mm, eps_op, last_rstd_reader[0]])
        bank_last[6 + half] = ln
        rexp[half] = P.add("act", lambda e, c0=c0, c1=c1: e.activation(out=rstd_bc[:, c0:c1], in_=rstd_bc[:, c0:c1], func=AF.Exp, scale=-0.5), [ln])
    for k in range(KC):
        final_half_out(k, 1)

    keys = P.finalize()
    with ExitStack() as es:
        sems = {k: es.enter_context(nc.semaphore("s_" + k)) for k in keys}
        block = es.enter_context(nc.Block())

        @block.tensor
        def _(e):
            P.emit("pe", e, sems)

        @block.scalar
        def _(e):
            P.emit("act", e, sems)

        @block.vector
        def _(e):
            P.emit("dve", e, sems)

        @block.gpsimd
        def _(e):
            P.emit("pool", e, sems)

        @block.sync
        def _(e):
            P.emit("sp", e, sems)
            e.wait_ge(sems["dma_st"], P.dma_counts["dma_st"])
    return nc


_CACHE = {}


def _host_inputs(x, g_mix, w_in, g_v, w_s, b_s, w_pool, pool_scale, w_out, g_ffn, w_up, w_down, g_final):
    f32 = np.float32
    x = np.asarray(x, f32)
    w_in2 = np.ascontiguousarray(np.asarray(w_in, f32)[0])
    w_out2 = np.ascontiguousarray(np.asarray(w_out, f32)[0])
    w_up2 = np.ascontiguousarray(np.asarray(w_up, f32)[0])
    w_down2 = np.ascontiguousarray(np.asarray(w_down, f32)[0])
    wsT = np.ascontiguousarray(np.asarray(w_s, f32)[0].transpose(2, 0, 1))
    s_idx = np.arange(128)[:, None, None]
    t_idx = np.arange(128)[None, None, :]
    mask = np.ascontiguousarray(np.broadcast_to((s_idx <= t_idx), (128, 8, 128)).astype(f32))
    wpool = np.ascontiguousarray(np.asarray(w_pool, f32)[0].reshape(4, 2, 128, 256).transpose(2, 0, 1, 3))
    gvt = np.ascontiguousarray(np.broadcast_to(np.asarray(g_v, f32)[0][None, :], (128, 1024)))
    bst = np.ascontiguousarray(np.broadcast_to(np.asarray(b_s, f32)[0].reshape(1, 1024), (128, 1024)))
    vecs = np.zeros((128, 56), f32)
    vecs[:, 0:16] = np.asarray(g_mix, f32)[0].reshape(16, 128).T
    vecs[:, 16:32] = np.asarray(g_ffn, f32)[0].reshape(16, 128).T
    vecs[:, 32:48] = np.asarray(g_final, f32).reshape(16, 128).T
    vecs[:, 48:56] = np.asarray(pool_scale, f32)[0].reshape(8, 128).T
    common = {"w_in": w_in2, "w_out": w_out2, "w_up": w_up2, "w_down": w_down2, "wsT": wsT, "mask": mask,
              "wpool": wpool, "gvt": gvt, "bst": bst, "vecs": vecs}
    in_maps = []
    wins = np.array([2, 4, 8, 16], f32)
    for c in range(NCORES):
        b, q = divmod(c, 4)
        t0 = q * TOK
        xT = np.ascontiguousarray(x[b, t0:t0 + TOK, :].T)
        if q == 0:
            xh = np.zeros((D, HALO), f32)
            cnt = np.minimum(np.arange(HALO, dtype=f32)[None, :] + 1.0, wins[:, None])
        else:
            xh = np.ascontiguousarray(x[b, t0 - HALO:t0, :].T)
            cnt = np.broadcast_to(wins[:, None], (4, HALO))
        invc = np.ascontiguousarray(np.broadcast_to((1.0 / cnt).astype(f32)[None], (128, 4, HALO)))
        m = dict(common)
        m["xT"] = xT
        m["xh"] = xh
        m["invc"] = invc
        in_maps.append(m)
    return in_maps


def kernel(x, g_mix, w_in, g_v, w_s, b_s, w_pool, pool_scale, w_out, g_ffn, w_up, w_down, g_final):
    in_maps = _host_inputs(x, g_mix, w_in, g_v, w_s, b_s, w_pool, pool_scale, w_out, g_ffn, w_up, w_down, g_final)
    nc = build_program()
    res = run_bass_kernel_spmd(nc, in_maps, core_ids=list(range(NCORES)))
    out = np.empty((2, SEQ, D), np.float32)
    for c in range(NCORES):
        b, q = divmod(c, 4)
        out[b, q * TOK:(q + 1) * TOK, :] = np.asarray(res.results[c]["outT"], np.float32).T
    return out
```

```python
from contextlib import ExitStack

import numpy as np
import concourse.bass as bass
import concourse.mybir as mybir
from concourse.bass_utils import run_bass_kernel_spmd

F32 = mybir.dt.float32
BF16 = mybir.dt.bfloat16
AF = mybir.ActivationFunctionType
ALU = mybir.AluOpType
AX = mybir.AxisListType

NCORES = 8
D = 2048
TOK = 1024
SEQ = 4096
KC = 16
HALO = 16
ZL = TOK + HALO
DFF = 8192
EPS = 1e-6
NSLOT = 5
SLABW = 256
FB = 16
NFB = DFF // (FB * 128)


class Op:
    __slots__ = ("eng", "fn", "deps", "kind", "signal", "semkey", "val")

    def __init__(self, eng, fn, deps, kind, semkey):
        self.eng = eng
        self.fn = fn
        self.deps = [d for d in deps if d is not None]
        self.kind = kind
        self.signal = False
        self.semkey = semkey
        self.val = None


class Prog:
    ENGS = ("pe", "act", "dve", "pool", "sp")

    def __init__(self):
        self.ops = {e: [] for e in self.ENGS}
        self.dma_counts = {}

    def add(self, eng, fn, deps=()):
        op = Op(eng, fn, deps, "c", eng)
        for d in op.deps:
            d.signal = True
        self.ops[eng].append(op)
        return op

    def dma(self, queue, fn, semname, deps=()):
        op = Op(queue, fn, deps, "d", "dma_" + semname)
        for d in op.deps:
            d.signal = True
        cnt = self.dma_counts.get(op.semkey, 0) + 16
        self.dma_counts[op.semkey] = cnt
        op.val = cnt
        self.ops[queue].append(op)
        return op

    def finalize(self):
        for e in self.ENGS:
            c = 0
            for op in self.ops[e]:
                if op.kind == "c" and op.signal:
                    c += 1
                    op.val = c
        return sorted(set(self.ENGS) | set(self.dma_counts.keys()))

    def emit(self, eng, e, sems):
        waited = {}
        for op in self.ops[eng]:
            for d in op.deps:
                if d.kind == "c" and d.eng == eng and eng in ("pe", "sp"):
                    continue
                assert d.val is not None
                if waited.get(d.semkey, 0) >= d.val:
                    continue
                e.wait_ge(sems[d.semkey], d.val)
                waited[d.semkey] = d.val
            ins = op.fn(e)
            if op.kind == "d":
                ins.then_inc(sems[op.semkey], 16)
            elif op.signal:
                ins.then_inc(sems[eng], 1)


def build_program(debug=False):
    nc = bass.Bass("TRN2", target_bir_lowering=False)
    P = Prog()

    def dram_in(name, shape):
        return nc.dram_tensor(name, list(shape), F32, kind="ExternalInput").ap()

    xT_d = dram_in("xT", [D, TOK]).rearrange("(k p) t -> p k t", p=128)
    xh_d = dram_in("xh", [D, HALO]).rearrange("(k p) t -> p k t", p=128)
    w_in_d = dram_in("w_in", [D, 3072]).rearrange("(k p) n -> p k n", p=128)
    w_out_d = dram_in("w_out", [D, D]).rearrange("(k p) n -> p k n", p=128)
    w_up_d = dram_in("w_up", [D, DFF]).rearrange("(k p) n -> p k n", p=128)
    w_down_d = dram_in("w_down", [DFF, D]).rearrange("(k p) n -> p k n", p=128)
    wsT_d = dram_in("wsT", [128, 8, 128])
    mask_d = dram_in("mask", [128, 8, 128])
    wpool_d = dram_in("wpool", [128, 4, 2, 256])
    gv_d = dram_in("gvt", [128, 1024])
    bs_d = dram_in("bst", [128, 1024])
    vecs_d = dram_in("vecs", [128, 56])
    invc_d = dram_in("invc", [128, 4, 16])
    out_d = nc.dram_tensor("outT", [D, TOK], F32, kind="ExternalOutput").ap()

    B0 = 16512
    X0 = B0
    H0 = X0 + 66048
    S0 = H0 + 32768
    R0 = S0 + 33024
    C0 = R0 + NSLOT * 8192
    cur = [C0]

    def sb(name, shape, dt, off):
        return nc.alloc_sbuf_tensor_at(name, list(shape), dt, offset=off)

    def calloc(name, shape, dt, nbytes):
        off = cur[0]
        cur[0] += (nbytes + 31) // 32 * 32
        assert cur[0] <= 229376, cur[0]
        return sb(name, shape, dt, off)

    xf = sb("xf", [128, KC, TOK], F32, X0)
    uT = sb("uT", [128, 8, TOK], BF16, X0)
    zE = sb("zE", [128, 8, ZL], F32, X0 + 16384)
    pooled = sb("pooled", [128, 8, TOK], BF16, X0 + 49664)
    xr_mid = sb("xr_mid", [128, 8, TOK], F32, X0 + 16384)
    xr_h = sb("xr_h", [128, 8, TOK], F32, H0)
    hT = sb("hT", [128, KC, TOK], BF16, H0)
    vh = sb("vh", [128, 8, TOK], BF16, S0)
    vf = sb("vf", [128, 2, TOK], F32, S0 + 16384)
    sqv = sb("sqv", [128, 1024], F32, S0 + 24576)
    mtmp = sb("mtmp", [128, 2, 512], F32, S0 + 28672)
    h2T = sb("h2T", [128, KC, TOK], BF16, S0)
    rsr = sb("rsr", [1, TOK], F32, S0 + 16384)
    rs_hi = sb("rs_hi", [1, TOK], BF16, S0 + 20480)
    rs_lo = sb("rs_lo", [1, TOK], BF16, S0 + 22528)
    rs_lo2 = sb("rs_lo2", [1, TOK], BF16, S0 + 24576)
    ringall = sb("ringall", [128, NSLOT, KC, SLABW], BF16, R0)

    gv_t = calloc("gv_t", [128, 1024], F32, 4096)
    rtmp = sb("rtmp", [128, 2, 512], F32, C0)
    bs_t = calloc("bs_t", [128, 1024], F32, 4096)
    rstd_bc = calloc("rstd_bc", [128, TOK], F32, 4096)
    wpool = calloc("wpool", [128, 4, 2, 256], BF16, 4096)
    wsT = calloc("wsT", [128, 8, 128], BF16, 2048)
    maskt = calloc("maskt", [128, 8, 128], BF16, 2048)
    sqr = calloc("sqr", [128, 2, TOK], BF16, 4096)
    xst_off = cur[0]
    xstage = [calloc("xstage%d" % i, [128, TOK], F32, 4160) for i in range(2)]
    tmpA = sb("tmpA", [128, ZL], F32, xst_off)
    tmpB = sb("tmpB", [128, ZL], F32, xst_off + 4160)
    ones = calloc("ones", [128, 128], BF16, 256)
    vecs = calloc("vecs", [128, 56], F32, 224)
    invc = calloc("invc", [128, 4, 16], F32, 256)
    xh = calloc("xh", [128, KC, HALO], F32, 1024)
    hh = calloc("hh", [128, KC, HALO], BF16, 512)
    sqh = calloc("sqh", [128, KC, HALO], BF16, 512)
    rstd_h = calloc("rstd_h", [128, HALO], F32, 64)
    vstat = calloc("vstat", [128, 2, 8], F32, 64)
    vrs = calloc("vrs", [128, 2, 8], F32, 64)
    ptmp = calloc("ptmp", [128, 16], F32, 64)
    epst = calloc("epst", [128, 1], F32, 32)
    onef = calloc("onef", [128, 1], F32, 32)
    rstd_col = calloc("rstd_col", [128, 8], F32, 32)

    G_MIX, G_FFN, G_FIN, PSC = 0, 16, 32, 48

    def MIX(k):
        return uT[:, k, :] if k < 8 else pooled[:, k - 8, :]

    def XR(m):
        if 4 <= m < 12:
            return xr_mid[:, m - 4, :]
        return xr_h[:, m if m < 4 else m - 8, :]

    def RS(n, k, a, b):
        return ringall[:, n % NSLOT, k, a:b]

    ps = nc.alloc_psum_tensor("ps", [128, 8, 512], F32)
    NRING = 6
    bank_last = [None] * 8
    bank_next = [0]

    def take_bank():
        b = bank_next[0]
        bank_next[0] = (b + 1) % NRING
        return b

    def pe_job(segments, nb=2):
        banks = [take_bank() for _ in range(nb)]
        first = True
        last = None
        for seg in segments:
            if callable(seg):
                seg()
                continue
            mms, deps = seg
            d = list(deps() if callable(deps) else deps)
            if first:
                d += [bank_last[b] for b in banks]
                first = False

            def fn(e, mms=mms):
                ins = None
                for (sel, c0, c1, lhsT, rhs, st, sp_) in mms:
                    ins = e.matmul(ps[:, banks[sel], c0:c1], lhsT, rhs, start=st, stop=sp_)
                return ins
            last = P.add("pe", fn, d)
        return last, banks

    def proj_mms(lhs_fn, rhs_fn, ks, k_first, k_last):
        mms = []
        for k in ks:
            mms.append((0, 0, 512, lhs_fn(k), rhs_fn(k)[:, 0:512], k == k_first, k == k_last))
            mms.append((1, 0, 512, lhs_fn(k), rhs_fn(k)[:, 512:1024], k == k_first, k == k_last))
        return mms

    vecs_ld = P.dma("sp", lambda e: e.dma_start(out=vecs[:, :], in_=vecs_d[:, :]), "cst0")
    slab0a = [P.dma("pool", lambda e: e.dma_start(out=ringall[:, 0, :, 0:128], in_=w_in_d[:, :, 2048:2048 + 128]), "slot0a")]
    x_ld = []
    for q in range(4):
        x_ld.append(P.dma("sp", lambda e, q=q: e.dma_start(out=xf[:, 4 * q:4 * q + 4, :], in_=xT_d[:, 4 * q:4 * q + 4, :]),
                          "x%d" % q, [slab0a[0]]))
    cst1 = None
    for fn in (lambda e: e.dma_start(out=xh[:, :, :], in_=xh_d[:, :, :]),
               lambda e: e.dma_start(out=invc[:, :, :], in_=invc_d[:, :, :])):
        cst1 = P.dma("sp", fn, "cst1")

    slabs = []
    for j in range(4):
        slabs.append(w_in_d[:, :, 2048 + SLABW * j:2048 + SLABW * (j + 1)])
    for j in range(4):
        slabs.append(w_in_d[:, :, SLABW * j:SLABW * (j + 1)])
    for j in range(4):
        slabs.append(w_in_d[:, :, 1024 + SLABW * j:1024 + SLABW * (j + 1)])
    for _rep in range(2):
        for j in range(8):
            slabs.append(w_out_d[:, :, SLABW * j:SLABW * (j + 1)])
    for fb in range(NFB):
        for j in range(8):
            c0 = fb * FB * 128 + SLABW * j
            slabs.append(w_up_d[:, :, c0:c0 + SLABW])
        for j in range(8):
            slabs.append(w_down_d[:, fb * FB:(fb + 1) * FB, SLABW * j:SLABW * (j + 1)])
    for j in range(8):
        slabs.append(w_down_d[:, (NFB - 1) * FB:NFB * FB, SLABW * j:SLABW * (j + 1)])
    NSLAB = len(slabs)
    slab_ld = [None] * NSLAB
    slab_done = [None] * NSLAB
    nxt = [0]

    def issue_loads(upto):
        upto = min(upto, NSLAB - 1)
        while nxt[0] <= upto:
            n = nxt[0]
            deps = []
            if n >= NSLOT:
                assert slab_done[n - NSLOT] is not None, n
                deps.append(slab_done[n - NSLOT])
            else:
                deps.append(x_ld[3])
            if n == 0:
                slab_ld[0] = P.dma("pool", lambda e: e.dma_start(out=ringall[:, 0, :, 128:256], in_=slabs[0][:, :, 128:256]), "slot0", deps)
            else:
                slab_ld[n] = P.dma("pool", lambda e, n=n: e.dma_start(out=ringall[:, n % NSLOT, :, :], in_=slabs[n]),
                                   "slot%d" % (n % NSLOT), deps)
            nxt[0] += 1

    issue_loads(NSLOT - 1)
    cstp = None
    for fn in (lambda e: e.dma_start(out=wsT[:, :, :], in_=wsT_d[:, :, :]),
               lambda e: e.dma_start(out=maskt[:, :, :], in_=mask_d[:, :, :]),
               lambda e: e.dma_start(out=wpool[:, :, :, :], in_=wpool_d[:, :, :, :])):
        cstp = P.dma("pool", fn, "cstp", [x_ld[3]])
    cst = None
    for fn in (lambda e: e.dma_start(out=gv_t[:, :], in_=gv_d[:, :]),
               lambda e: e.dma_start(out=bs_t[:, :], in_=bs_d[:, :])):
        cst = P.dma("sp", fn, "cst", [slab_ld[NSLOT - 1]])

    ones_op = P.add("dve", lambda e: e.memset(ones[:, :], 1.0))
    eps_op = P.add("dve", lambda e: e.memset(epst[:, :], EPS))
    onef_op = P.add("dve", lambda e: e.memset(onef[:, :], 1.0))
    EPSB = epst[:, 0:1]

    sq_reader = [None, None]

    def stat_chunk(k, src_ap, src_deps, eng):
        s = k % 2
        if eng == "act":
            sq = P.add("act", lambda e: e.activation(out=sqr[:, s, :], in_=src_ap, func=AF.Square), list(src_deps) + [sq_reader[s]])
        else:
            sq = P.add("dve", lambda e: e.tensor_tensor(out=sqr[:, s, :], in0=src_ap, in1=src_ap, op=ALU.mult), list(src_deps) + [sq_reader[s]])

        def mk():
            def fn(e):
                e.matmul(ps[:, 6, :], ones[:, :], sqr[:, s, 0:512], start=(k == 0), stop=(k == KC - 1))
                return e.matmul(ps[:, 7, :], ones[:, :], sqr[:, s, 512:1024], start=(k == 0), stop=(k == KC - 1))
            d = [sq, ones_op]
            if k == 0:
                d += [bank_last[6], bank_last[7]]
            pe = P.add("pe", fn, d)
            sq_reader[s] = pe
            return pe
        return sq, mk

    def rstd_chain(pe_op, war_deps):
        s1 = P.add("act", lambda e: e.activation(out=rstd_bc[:, 0:512], in_=ps[:, 6, :], func=AF.Ln, bias=EPSB, scale=1.0 / D),
                   [pe_op, eps_op] + list(war_deps))
        s2 = P.add("act", lambda e: e.activation(out=rstd_bc[:, 512:1024], in_=ps[:, 7, :], func=AF.Ln, bias=EPSB, scale=1.0 / D),
                   [pe_op, eps_op] + list(war_deps))
        bank_last[6] = s1
        bank_last[7] = s2
        P.add("act", lambda e: e.activation(out=rstd_bc[:, 0:512], in_=rstd_bc[:, 0:512], func=AF.Exp, scale=-0.5), [s1])
        return P.add("act", lambda e: e.activation(out=rstd_bc[:, 512:1024], in_=rstd_bc[:, 512:1024], func=AF.Exp, scale=-0.5), [s2])

    h_ops = [None] * KC
    n1_mm = [None]
    z_evac = [None] * 8
    zh_last = [None]
    last_rstd_reader = [None]

    def z_evacs(m, job, banks, hjob):
        e1 = P.add("dve", lambda e: e.tensor_tensor(out=zE[:, m, HALO:HALO + 512], in0=ps[:, banks[0], :], in1=rstd_bc[:, 0:512], op=ALU.mult), [job, r1[0]])
        e2 = P.add("dve", lambda e: e.tensor_tensor(out=zE[:, m, HALO + 512:ZL], in0=ps[:, banks[1], :], in1=rstd_bc[:, 512:1024], op=ALU.mult), [job, r1[0]])
        e3 = P.add("dve", lambda e: e.tensor_tensor(out=zE[:, m, 0:HALO], in0=ps[:, 7, m * HALO:(m + 1) * HALO], in1=rstd_h[:, :], op=ALU.mult), [hjob, hs2])
        bank_last[banks[0]] = e1
        bank_last[banks[1]] = e2
        bank_last[7] = e3
        z_evac[m] = (e1, e2, e3)
        last_rstd_reader[0] = e2

    def z_halo_job(m, n, mmi):
        def halo_fn(e):
            ins = None
            for k in range(KC):
                ins = e.matmul(ps[:, 7, m * HALO:(m + 1) * HALO], RS(n, k, mmi * 128, (mmi + 1) * 128), hh[:, k, :],
                               start=(k == 0), stop=(k == KC - 1))
            return ins
        return P.add("pe", halo_fn, [slab_ld[n] if (n, mmi) != (0, 0) else slab0a[0], hh_all, bank_last[7]])

    segs = []
    for k in range(KC):
        def hook(k=k):
            xk = xf[:, k, :]
            ld = x_ld[k // 4]
            if k % 2 == 0:
                h_ops[k] = P.add("act", lambda e: e.activation(out=hT[:, k, :], in_=xk, func=AF.Copy, scale=vecs[:, G_MIX + k:G_MIX + k + 1]), [ld, vecs_ld])
                sq, mk = stat_chunk(k, xk, [ld], "dve")
            else:
                h_ops[k] = P.add("dve", lambda e: e.tensor_scalar(out=hT[:, k, :], in0=xk, scalar1=vecs[:, G_MIX + k:G_MIX + k + 1], scalar2=None, op0=ALU.mult), [ld, vecs_ld])
                sq, mk = stat_chunk(k, xk, [ld], "act")
            n1_mm[0] = mk()
        segs.append(hook)
        segs.append((proj_mms(lambda kk: RS(0, kk, 0, 128), lambda kk: hT[:, kk, :], [k], 0, KC - 1), lambda k=k: [slab0a[0], h_ops[k]]))
    zjob0, zbanks0 = pe_job(segs)
    r1 = [rstd_chain(n1_mm[0], [])]

    sqh_op = P.add("dve", lambda e: e.tensor_tensor(out=sqh[:, :, :], in0=xh[:, :, :], in1=xh[:, :, :], op=ALU.mult), [cst1])
    bh = take_bank()

    def halo_stat(e):
        ins = None
        for k in range(KC):
            ins = e.matmul(ps[:, bh, 0:HALO], ones[:, :], sqh[:, k, :], start=(k == 0), stop=(k == KC - 1))
        return ins
    hs_pe = P.add("pe", halo_stat, [sqh_op, ones_op, bank_last[bh]])
    hs1 = P.add("act", lambda e: e.activation(out=rstd_h[:, :], in_=ps[:, bh, 0:HALO], func=AF.Ln, bias=EPSB, scale=1.0 / D), [hs_pe, eps_op])
    bank_last[bh] = hs1
    hs2 = P.add("act", lambda e: e.activation(out=rstd_h[:, :], in_=rstd_h[:, :], func=AF.Exp, scale=-0.5), [hs1])

    h_all = [h_ops[KC - 1], h_ops[KC - 2]]

    hh_ops = []
    for k in range(KC):
        hh_ops.append(P.add("dve", lambda e, k=k: e.tensor_scalar(out=hh[:, k, :], in0=xh[:, k, :], scalar1=vecs[:, G_MIX + k:G_MIX + k + 1],
                                                                   scalar2=None, op0=ALU.mult), [cst1, vecs_ld]))
    hh_all = hh_ops[-1]

    pool_ops = [None] * 8
    prev_tmp_user = [None]

    def pooling_chunk(c):
        g = c // 2
        win = 2 << g
        E = zE[:, c, :]
        deps0 = list(z_evac[c]) + [prev_tmp_user[0]]
        o = P.add("dve", lambda e: e.tensor_tensor(out=tmpA[:, 1:ZL], in0=E[:, 1:ZL], in1=E[:, 0:ZL - 1], op=ALU.add), deps0)
        src, dst = tmpA, tmpB
        sh, lo = 2, 1
        while sh < win:
            lo2 = lo + sh
            o = P.add("dve", lambda e, src=src, dst=dst, sh=sh, lo2=lo2: e.tensor_tensor(
                out=dst[:, lo2:ZL], in0=src[:, lo2:ZL], in1=src[:, lo2 - sh:ZL - sh], op=ALU.add), [o])
            src, dst = dst, src
            lo = lo2
            sh *= 2
        assert lo <= HALO
        fin = P.add("dve", lambda e, src=src: e.scalar_tensor_tensor(
            out=pooled[:, c, :], in0=src[:, HALO:ZL], scalar=1.0 / win, in1=E[:, HALO:ZL], op0=ALU.mult, op1=ALU.subtract), [o])
        f1 = P.add("dve", lambda e, src=src: e.tensor_tensor(out=ptmp[:, :], in0=src[:, HALO:2 * HALO], in1=invc[:, g, :], op=ALU.mult), [fin, cst1])
        f2 = P.add("dve", lambda e: e.tensor_tensor(out=pooled[:, c, 0:HALO], in0=ptmp[:, :], in1=E[:, HALO:2 * HALO], op=ALU.subtract), [f1])
        prev_tmp_user[0] = f2
        pool_ops[c] = f2

    c1 = P.add("dve", lambda e: e.tensor_copy(out=rs_hi[:, :], in_=rstd_bc[0:1, :]), [r1[0]])
    c2 = P.add("dve", lambda e: e.tensor_tensor(out=rsr[:, :], in0=rstd_bc[0:1, :], in1=rs_hi[:, :], op=ALU.subtract), [c1])
    c3 = P.add("dve", lambda e: e.tensor_copy(out=rs_lo[:, :], in_=rsr[:, :]), [c2])
    c4 = P.add("dve", lambda e: e.tensor_tensor(out=rsr[:, :], in0=rsr[:, :], in1=rs_lo[:, :], op=ALU.subtract), [c3])
    c5 = P.add("dve", lambda e: e.tensor_copy(out=rs_lo2[:, :], in_=rsr[:, :]), [c4])
    hj = z_halo_job(0, 0, 0)
    z_evacs(0, zjob0, zbanks0, hj)
    rc_evac = None
    for m in range(1, 8):
        n, mmi = m // 2, m % 2
        if mmi == 0:
            issue_loads(n + NSLOT - 1)
        job, banks = pe_job([(proj_mms(lambda k: RS(n, k, mmi * 128, (mmi + 1) * 128), lambda k: hT[:, k, :], range(KC), 0, KC - 1),
                              [slab_ld[n]] + h_all)])
        hj = z_halo_job(m, n, mmi)
        slab_done[n] = hj
        z_evacs(m, job, banks, hj)
        if m % 2 == 1:
            pooling_chunk(m // 2)
        if m == 1:
            slab_done[0] = hj
        if m == 3:
            bc = take_bank()

            def col_fn(e, bc=bc):
                ins = None
                first = True
                for i in range(8):
                    for t in (rs_hi, rs_lo, rs_lo2):
                        ins = e.matmul(ps[:, bc, i:i + 1], t[0:1, i * 128:(i + 1) * 128], ones[0:1, 0:1], start=first, stop=(i == 7 and t is rs_lo2))
                        first = False
                return ins
            col_pe = P.add("pe", col_fn, [c1, c3, c5, ones_op, bank_last[bc]])
            rc_evac = P.add("act", lambda e, bc=bc: e.activation(out=rstd_col[:, :], in_=ps[:, bc, 0:8], func=AF.Copy), [col_pe])
            bank_last[bc] = rc_evac

    ut_reader = [None, None]
    ut_ctr = [0]
    u_evac_last = None
    for m in range(8):
        n, mmi = 4 + m // 2, m % 2
        if mmi == 0:
            issue_loads(n + NSLOT - 1)
        job, banks = pe_job([(proj_mms(lambda k: RS(n, k, mmi * 128, (mmi + 1) * 128), lambda k: hT[:, k, :], range(KC), 0, KC - 1),
                              [slab_ld[n]] + h_all)])
        slab_done[n] = job
        for half in range(2):
            s = ut_ctr[0] % 2
            ut_ctr[0] += 1
            a1 = P.add("dve", lambda e, s=s, half=half, b=banks[half]: e.tensor_tensor(
                out=mtmp[:, s, :], in0=ps[:, b, :], in1=rstd_bc[:, half * 512:(half + 1) * 512], op=ALU.mult), [job, r1[0], ut_reader[s]])
            bank_last[banks[half]] = a1
            a2 = P.add("act", lambda e, s=s, half=half, m=m: e.activation(out=uT[:, m, half * 512:(half + 1) * 512], in_=mtmp[:, s, :], func=AF.Gelu_apprx_tanh), [a1])
            ut_reader[s] = a2
            u_evac_last = a2
            last_rstd_reader[0] = a1
        if m % 2 == 0:
            pooling_chunk(4 + m // 2)

    pool_all = pool_ops[7]

    ws_op = P.add("dve", lambda e: e.tensor_tensor(out=wsT[:, :, :], in0=wsT[:, :, :], in1=maskt[:, :, :], op=ALU.mult), [cstp])
    issue_loads(12)
    v_deps = [slab_ld[8 + j] for j in range(4)]
    vf_readers = [None, None]
    vst_readers = [None, None]
    vh_ops = [None] * 8
    v_jobs = []
    mix_last = [None]
    mix_tile_last = [None] * 8
    mt_readers = [ut_reader[0], ut_reader[1]]
    mt_ctr = [0]

    def emit_mixing(i):
        for hb in range(2):
            ba = take_bank()

            def fn(e, ba=ba, hb=hb):
                ins = None
                for hq in range(4):
                    h = 4 * hb + hq
                    ins = e.matmul(ps[:, ba, hq * 128:(hq + 1) * 128], vh[:, i, h * 128:(h + 1) * 128], wsT[:, h, :], start=True, stop=True)
                return ins
            job = P.add("pe", fn, [vh_ops[i], ws_op, bank_last[ba]])
            s = mt_ctr[0] % 2
            mt_ctr[0] += 1
            a1 = P.add("dve", lambda e, s=s, ba=ba, hb=hb: e.tensor_tensor(
                out=mtmp[:, s, :], in0=ps[:, ba, :], in1=bs_t[:, 512 * hb:512 * hb + 512], op=ALU.add), [job, cst, mt_readers[s]])
            bank_last[ba] = a1
            a2 = P.add("dve", lambda e, s=s, hb=hb: e.tensor_tensor(
                out=uT[:, 4 * hb:4 * hb + 4, i * 128:(i + 1) * 128],
                in0=mtmp[:, s, :].rearrange("p (h t) -> p h t", h=4),
                in1=uT[:, 4 * hb:4 * hb + 4, i * 128:(i + 1) * 128], op=ALU.mult), [a1, u_evac_last])
            mt_readers[s] = a2
            mix_last[0] = a2
            mix_tile_last[i] = a2

    ob_last = [None]
    ob_last_dve = [None]

    def emit_pool_mm(g):
        if True:
            jobs = []
            for mmi in range(2):
                mms = []
                for kc in range(2):
                    lhsT = wpool[:, g, kc, mmi * 128:(mmi + 1) * 128]
                    mms.append((0, 0, 512, lhsT, pooled[:, 2 * g + kc, 0:512], kc == 0, kc == 1))
                    mms.append((1, 0, 512, lhsT, pooled[:, 2 * g + kc, 512:1024], kc == 0, kc == 1))
                jobs.append(pe_job([(mms, [cstp, pool_ops[2 * g], pool_ops[2 * g + 1]])]))
            for mmi in range(2):
                job, banks = jobs[mmi]
                c = 2 * g + mmi
                ev0 = P.add("act", lambda e, c=c, b=banks[0]: e.activation(
                    out=pooled[:, c, 0:512], in_=ps[:, b, :], func=AF.Copy, scale=vecs[:, PSC + c:PSC + c + 1]),
                    [jobs[0][0], jobs[1][0], cst])
                ev1 = P.add("dve", lambda e, c=c, b=banks[1]: e.tensor_scalar(
                    out=pooled[:, c, 512:1024], in0=ps[:, b, :], scalar1=vecs[:, PSC + c:PSC + c + 1], scalar2=None, op0=ALU.mult),
                    [jobs[0][0], jobs[1][0], cst])
                bank_last[banks[0]] = ev0
                bank_last[banks[1]] = ev1
                ob_last[0] = ev0
                ob_last_dve[0] = ev1

    for i in range(8):
        mms = []
        for k in range(KC):
            lhsT = hT[:, k, i * 128:(i + 1) * 128]
            mms.append((0, 0, 512, lhsT, ringall[:, 3:5, k, :], k == 0, k == KC - 1))
            mms.append((1, 0, 512, lhsT, ringall[:, 0:2, k, :], k == 0, k == KC - 1))
        job, banks = pe_job([(mms, v_deps + h_all)])
        v_jobs.append(job)
        s = i % 2
        g1 = P.add("act", lambda e, s=s, i=i, b=banks[0]: e.activation(out=vf[:, s, 0:512], in_=ps[:, b, :], func=AF.Gelu_apprx_tanh,
                                                                        scale=rstd_col[:, i:i + 1]), [job, rc_evac, vf_readers[s]])
        g2 = P.add("act", lambda e, s=s, i=i, b=banks[1]: e.activation(out=vf[:, s, 512:1024], in_=ps[:, b, :], func=AF.Gelu_apprx_tanh,
                                                                        scale=rstd_col[:, i:i + 1]), [job, rc_evac, vf_readers[s]])
        bank_last[banks[0]] = g1
        bank_last[banks[1]] = g2
        if 3 <= i <= 6:
            emit_pool_mm(i - 3)
        q2 = None
        for h in range(8):
            q2 = P.add("act", lambda e, s=s, h=h: e.activation(out=sqv[:, h * 128:(h + 1) * 128], in_=vf[:, s, h * 128:(h + 1) * 128], func=AF.Square,
                                                              accum_out=vstat[:, s, h:h + 1]), [g1, g2, vst_readers[s]])
        q3 = P.add("act", lambda e, s=s: e.activation(out=vrs[:, s, :], in_=vstat[:, s, :], func=AF.Ln, bias=EPSB, scale=1.0 / 128), [q2, eps_op])
        q4 = P.add("act", lambda e, s=s: e.activation(out=vrs[:, s, :], in_=vrs[:, s, :], func=AF.Exp, scale=-0.5), [q3])
        if i >= 2:
            emit_mixing(i - 2)
        if i == 7:
            emit_mixing(6)
        last = None
        for h in range(8):
            last = P.add("dve", lambda e, s=s, h=h, i=i: e.scalar_tensor_tensor(
                out=vh[:, i, h * 128:(h + 1) * 128], in0=vf[:, s, h * 128:(h + 1) * 128], scalar=vrs[:, s, h:h + 1],
                in1=gv_t[:, h * 128:(h + 1) * 128], op0=ALU.mult, op1=ALU.mult), [q4, cst])
        vf_readers[s] = last
        vst_readers[s] = last
        vh_ops[i] = last
    for j in range(4):
        slab_done[8 + j] = v_jobs[-1]
    h_dead = v_jobs[-1]

    xs4 = [xstage[0][:, 0:512], xstage[0][:, 512:1024], xstage[1][:, 0:512], xstage[1][:, 512:1024]]
    xs_reader4 = [pool_all] * 4
    order = [(half, m) for half in range(2) for m in range(KC)]
    xs_ld = {}

    def load_xs(idx):
        half, m = order[idx]
        s = idx % 4
        xs_ld[idx] = P.dma("sp", lambda e: e.dma_start(out=xs4[s], in_=xT_d[:, m, half * 512:(half + 1) * 512]), "xs%d" % s, [xs_reader4[s]])

    for idx in range(4):
        load_xs(idx)
    x1_e = [[None] * KC, [None] * KC]
    h2h = [[None] * KC, [None] * KC]
    hs_reader = [sq_reader[0], sq_reader[0], sq_reader[1], sq_reader[1]]
    hs_ctr = [0]
    pend = []
    rexp2 = [None, None]
    h2_defer = []
    wo_last = None
    korder = list(range(8, 16)) + list(range(8))

    def flush_one():
        pe, half, m = pend.pop(0)()
        if m == KC - 1:
            c0, c1 = half * 512, (half + 1) * 512
            ln = P.add("act", lambda e: e.activation(out=rstd_bc[:, c0:c1], in_=ps[:, 6 + half, :], func=AF.Ln, bias=EPSB, scale=1.0 / D),
                       [pe, eps_op, last_rstd_reader[0]])
            bank_last[6 + half] = ln
            rexp2[half] = P.add("act", lambda e: e.activation(out=rstd_bc[:, c0:c1], in_=rstd_bc[:, c0:c1], func=AF.Exp, scale=-0.5), [ln])

    for idx, (half, m) in enumerate(order):
        c0, c1 = half * 512, (half + 1) * 512
        n, mmi = 12 + half * 8 + m // 2, m % 2
        if mmi == 0:
            issue_loads(n + NSLOT - 1)
        mms = [(0, 0, 512, RS(n, k, mmi * 128, (mmi + 1) * 128), MIX(k)[:, c0:c1], k == korder[0], k == korder[-1]) for k in korder]
        job, banks = pe_job([(mms, [slab_ld[n], ob_last[0], ob_last_dve[0], mix_tile_last[3 if half == 0 else 7]])], nb=1)
        slab_done[n] = job
        wo_last = job
        if idx == 2:
            emit_mixing(7)
            for f_ in h2_defer:
                f_()
            h2_defer = None
        while len(pend) > (3 if half == 0 else 1):
            flush_one()
        ev = P.add("dve", lambda e, m=m, b=banks[0], c0=c0, c1=c1, xs=xs4[idx % 4]: e.tensor_tensor(out=XR(m)[:, c0:c1], in0=ps[:, b, :], in1=xs, op=ALU.add),
                   [job, xs_ld[idx], h_dead, pool_all])
        bank_last[banks[0]] = ev
        x1_e[half][m] = ev
        xs_reader4[idx % 4] = ev
        if idx + 4 < len(order):
            load_xs(idx + 4)
        hsl = hs_ctr[0] % 4
        hs_ctr[0] += 1
        sq_ap = sqr[:, hsl // 2, (hsl % 2) * 512:(hsl % 2) * 512 + 512]
        sq = P.add("act", lambda e, m=m, sq_ap=sq_ap, c0=c0, c1=c1: e.activation(out=sq_ap, in_=XR(m)[:, c0:c1], func=AF.Square), [ev, hs_reader[hsl]])

        def mk(m=m, half=half, sq=sq, sq_ap=sq_ap, hsl=hsl):
            d = [sq, ones_op]
            if m == 0:
                d.append(bank_last[6 + half])
            pe = P.add("pe", lambda e: e.matmul(ps[:, 6 + half, :], ones[:, :], sq_ap, start=(m == 0), stop=(m == KC - 1)), d)
            hs_reader[hsl] = pe
            return pe, half, m
        pend.append(mk)

        def mk_h2(m=m, half=half, ev=ev, c0=c0, c1=c1):
            h2h[half][m] = P.add("act", lambda e: e.activation(out=h2T[:, m, c0:c1], in_=XR(m)[:, c0:c1], func=AF.Copy,
                                                              scale=vecs[:, G_FFN + m:G_FFN + m + 1]), [ev, mix_tile_last[7], cst])
        if h2_defer is not None:
            h2_defer.append(mk_h2)
        else:
            mk_h2()
    x1_ops = x1_e[1]
    h2_ops = h2h[1]

    rt_readers = [None, None]
    rt_ctr = [0]
    x2_ops = list(x1_ops)
    x2_e1 = list(x1_e[0])
    nbase = 28
    r2 = [None]
    prev_down_last = None
    fin_mk = [None] * KC
    n3_mm = [None]
    for fb in range(NFB):
        act_ops = [None] * FB
        lastblk = fb == NFB - 1
        for f in range(FB):
            n, mmi = nbase + fb * 16 + f // 2, f % 2
            if mmi == 0:
                issue_loads(n + NSLOT - 1)
            lhs = lambda k: RS(n, k, mmi * 128, (mmi + 1) * 128)
            rhs = lambda k: h2T[:, k, :]
            if fb == 0 and f == 0:
                segs = []
                for k in range(KC):
                    if k == KC - 1:
                        def hook():
                            while pend:
                                flush_one()
                        segs.append(hook)
                    segs.append((proj_mms(lhs, rhs, [k], 0, KC - 1), [slab_ld[n], h2_ops[k], wo_last]))
                job, banks = pe_job(segs)
                r2[0] = rexp2[1]
            else:
                job, banks = pe_job([(proj_mms(lhs, rhs, range(KC), 0, KC - 1), [slab_ld[n], h2_ops[KC - 1]])])
            slab_done[n] = job
            last = None
            for half in range(2):
                s = rt_ctr[0] % 2
                rt_ctr[0] += 1
                a1 = P.add("dve", lambda e, s=s, half=half, b=banks[half]: e.scalar_tensor_tensor(
                    out=rtmp[:, s, :], in0=ps[:, b, :], scalar=0.0, in1=rstd_bc[:, half * 512:(half + 1) * 512], op0=ALU.max, op1=ALU.mult),
                    [job, r2[0], rt_readers[s], vh_ops[7]])
                bank_last[banks[half]] = a1
                last_rstd_reader[0] = a1
                a2 = P.add("act", lambda e, s=s, f=f, half=half: e.activation(out=MIX(f)[:, half * 512:(half + 1) * 512], in_=rtmp[:, s, :], func=AF.Square),
                           [a1, wo_last, prev_down_last])
                rt_readers[s] = a2
                last = a2
            act_ops[f] = last
        if lastblk:
            break
        dj = None
        for m in range(KC):
            n, mmi = nbase + fb * 16 + 8 + m // 2, m % 2
            if mmi == 0:
                issue_loads(n + NSLOT - 1)
            lhs = lambda k: RS(n, k, mmi * 128, (mmi + 1) * 128)
            if m == 0:
                segs = [(proj_mms(lhs, MIX, [k], 0, FB - 1), [slab_ld[n], act_ops[k]]) for k in range(FB)]
            else:
                segs = [(proj_mms(lhs, MIX, range(FB), 0, FB - 1), [slab_ld[n], act_ops[FB - 1]])]
            job, banks = pe_job(segs)
            slab_done[n] = job
            dj = job
            e1 = P.add("dve", lambda e, m=m, b=banks[0]: e.tensor_tensor(out=XR(m)[:, 0:512], in0=XR(m)[:, 0:512], in1=ps[:, b, :], op=ALU.add),
                       [job, x2_ops[m], x2_e1[m], h2_ops[m]])
            e2 = P.add("dve", lambda e, m=m, b=banks[1]: e.tensor_tensor(out=XR(m)[:, 512:1024], in0=XR(m)[:, 512:1024], in1=ps[:, b, :], op=ALU.add),
                       [job, x2_ops[m], x2_e1[m], h2_ops[m]])
            bank_last[banks[0]] = e1
            bank_last[banks[1]] = e2
            x2_e1[m] = e1
            x2_ops[m] = e2
        prev_down_last = dj

    fb = NFB - 1
    pre_ops = [[None] * KC, [None] * KC]
    rexp = [None, None]

    def final_half_out(k, half):
        c0, c1 = half * 512, (half + 1) * 512
        o = P.add("dve", lambda e: e.tensor_tensor(out=XR(k)[:, c0:c1], in0=XR(k)[:, c0:c1], in1=rstd_bc[:, c0:c1], op=ALU.mult),
                  [rexp[half], pre_ops[half][k]])
        P.dma("sp", lambda e: e.dma_start(out=out_d[k * 128:(k + 1) * 128, c0:c1], in_=XR(k)[:, c0:c1]), "st", [o])

    for half in range(2):
        c0, c1 = half * 512, (half + 1) * 512
        pending = [None]
        last_mm = None
        for m in range(KC):
            n, mmi = nbase + fb * 16 + 8 + half * 8 + m // 2, m % 2
            if mmi == 0:
                issue_loads(n + NSLOT - 1)
            if half == 0 and m == 0:
                segs = [([(0, 0, 512, RS(n, k, mmi * 128, (mmi + 1) * 128), MIX(k)[:, c0:c1], k == 0, k == FB - 1)], [slab_ld[n], act_ops[k]])
                        for k in range(FB)]
            else:
                segs = [([(0, 0, 512, RS(n, k, mmi * 128, (mmi + 1) * 128), MIX(k)[:, c0:c1], k == 0, k == FB - 1) for k in range(FB)],
                         [slab_ld[n], act_ops[FB - 1]])]
            job, banks = pe_job(segs, nb=1)
            slab_done[n] = job
            if pending[0] is not None:
                last_mm = pending[0]()
            ev = P.add("dve", lambda e, m=m, b=banks[0], c0=c0, c1=c1: e.tensor_tensor(out=XR(m)[:, c0:c1], in0=XR(m)[:, c0:c1], in1=ps[:, b, :], op=ALU.add),
                       [job, x2_ops[m], x2_e1[m], h2_ops[m]])
            bank_last[banks[0]] = ev
            hsl = hs_ctr[0] % 4
            hs_ctr[0] += 1
            sq_ap = sqr[:, hsl // 2, (hsl % 2) * 512:(hsl % 2) * 512 + 512]
            sq = P.add("act", lambda e, m=m, sq_ap=sq_ap, c0=c0, c1=c1: e.activation(out=sq_ap, in_=XR(m)[:, c0:c1], func=AF.Square), [ev, hs_reader[hsl]])

            def mk(m=m, sq=sq, sq_ap=sq_ap, hsl=hsl, half=half):
                d = [sq, ones_op]
                if m == 0:
                    d.append(bank_last[6 + half])
                pe = P.add("pe", lambda e: e.matmul(ps[:, 6 + half, :], ones[:, :], sq_ap, start=(m == 0), stop=(m == KC - 1)), d)
                hs_reader[hsl] = pe
                return pe
            pending[0] = mk
            pre_ops[half][m] = P.add("act", lambda e, m=m, c0=c0, c1=c1: e.activation(out=XR(m)[:, c0:c1], in_=XR(m)[:, c0:c1], func=AF.Copy,
                                                                                      scale=vecs[:, G_FIN + m:G_FIN + m + 1]), [ev, sq, cst])
            if half == 1:
                final_half_out(m, 0)
        last_mm = pending[0]()
        ln = P.add("act", lambda e, half=half, c0=c0, c1=c1: e.activation(out=rstd_bc[:, c0:c1], in_=ps[:, 6 + half, :], func=AF.Ln, bias=EPSB, scale=1.0 / D),
                   [last_mm, eps_op, last_rstd_reader[0]])
        bank_last[6 + half] = ln
        rexp[half] = P.add("act", lambda e, c0=c0, c1=c1: e.activation(out=rstd_bc[:, c0:c1], in_=rstd_bc[:, c0:c1], func=AF.Exp, scale=-0.5), [ln])
    for k in range(KC):
        final_half_out(k, 1)

    keys = P.finalize()
    with ExitStack() as es:
        sems = {k: es.enter_context(nc.semaphore("s_" + k)) for k in keys}
        block = es.enter_context(nc.Block())

        @block.tensor
        def _(e):
            P.emit("pe", e, sems)

        @block.scalar
        def _(e):
            P.emit("act", e, sems)

        @block.vector
        def _(e):
            P.emit("dve", e, sems)

        @block.gpsimd
        def _(e):
            P.emit("pool", e, sems)

        @block.sync
        def _(e):
            P.emit("sp", e, sems)
            e.wait_ge(sems["dma_st"], P.dma_counts["dma_st"])
    return nc


_CACHE = {}


def _host_inputs(x, g_mix, w_in, g_v, w_s, b_s, w_pool, pool_scale, w_out, g_ffn, w_up, w_down, g_final):
    f32 = np.float32
    x = np.asarray(x, f32)
    w_in2 = np.ascontiguousarray(np.asarray(w_in, f32)[0])
    w_out2 = np.ascontiguousarray(np.asarray(w_out, f32)[0])
    w_up2 = np.ascontiguousarray(np.asarray(w_up, f32)[0])
    w_down2 = np.ascontiguousarray(np.asarray(w_down, f32)[0])
    wsT = np.ascontiguousarray(np.asarray(w_s, f32)[0].transpose(2, 0, 1))
    s_idx = np.arange(128)[:, None, None]
    t_idx = np.arange(128)[None, None, :]
    mask = np.ascontiguousarray(np.broadcast_to((s_idx <= t_idx), (128, 8, 128)).astype(f32))
    wpool = np.ascontiguousarray(np.asarray(w_pool, f32)[0].reshape(4, 2, 128, 256).transpose(2, 0, 1, 3))
    gvt = np.ascontiguousarray(np.broadcast_to(np.asarray(g_v, f32)[0][None, :], (128, 1024)))
    bst = np.ascontiguousarray(np.broadcast_to(np.asarray(b_s, f32)[0].reshape(1, 1024), (128, 1024)))
    vecs = np.zeros((128, 56), f32)
    vecs[:, 0:16] = np.asarray(g_mix, f32)[0].reshape(16, 128).T
    vecs[:, 16:32] = np.asarray(g_ffn, f32)[0].reshape(16, 128).T
    vecs[:, 32:48] = np.asarray(g_final, f32).reshape(16, 128).T
    vecs[:, 48:56] = np.asarray(pool_scale, f32)[0].reshape(8, 128).T
    common = {"w_in": w_in2, "w_out": w_out2, "w_up": w_up2, "w_down": w_down2, "wsT": wsT, "mask": mask,
              "wpool": wpool, "gvt": gvt, "bst": bst, "vecs": vecs}
    in_maps = []
    wins = np.array([2, 4, 8, 16], f32)
    for c in range(NCORES):
        b, q = divmod(c, 4)
        t0 = q * TOK
        xT = np.ascontiguousarray(x[b, t0:t0 + TOK, :].T)
        if q == 0:
            xh = np.zeros((D, HALO), f32)
            cnt = np.minimum(np.arange(HALO, dtype=f32)[None, :] + 1.0, wins[:, None])
        else:
            xh = np.ascontiguousarray(x[b, t0 - HALO:t0, :].T)
            cnt = np.broadcast_to(wins[:, None], (4, HALO))
        invc = np.ascontiguousarray(np.broadcast_to((1.0 / cnt).astype(f32)[None], (128, 4, HALO)))
        m = dict(common)
        m["xT"] = xT
        m["xh"] = xh
        m["invc"] = invc
        in_maps.append(m)
    return in_maps


def kernel(x, g_mix, w_in, g_v, w_s, b_s, w_pool, pool_scale, w_out, g_ffn, w_up, w_down, g_final):
    in_maps = _host_inputs(x, g_mix, w_in, g_v, w_s, b_s, w_pool, pool_scale, w_out, g_ffn, w_up, w_down, g_final)
    nc = build_program()
    res = run_bass_kernel_spmd(nc, in_maps, core_ids=list(range(NCORES)))
    out = np.empty((2, SEQ, D), np.float32)
    for c in range(NCORES):
        b, q = divmod(c, 4)
        out[b, q * TOK:(q + 1) * TOK, :] = np.asarray(res.results[c]["outT"], np.float32).T
    return out
```
